# Optimizing a Trainium2 kernel written in Bass

```python
import jax, jax.numpy as jnp
from jax import lax
import numpy as np

D_MODEL = 2048
BATCH = 8
SEQ = 4096
DEPTH = 4

CHUNK = 64
D_MIX = D_MODEL
CONV_CH = D_MIX // 2
N_HEADS = 8
HEAD_DIM = (D_MIX - CONV_CH) // N_HEADS
ATTN_W = N_HEADS * HEAD_DIM
CONV_K = 31
FFN_CONV_K = 3
D_FF = 5632
Q_BLOCK = 128
EPS = 1e-6
IN_COLS = 2 * CONV_CH + 3 * ATTN_W + N_HEADS

kernel_name = "hymba_conformer_fox_convffn_sandwich"


def rms_norm(x, g):
    xf = x.astype(jnp.float32)
    y = xf * lax.rsqrt(jnp.mean(xf * xf, axis=-1, keepdims=True) + EPS)
    return (y * g.astype(jnp.float32)).astype(x.dtype)


def layer_norm(x, g, b):
    xf = x.astype(jnp.float32)
    mu = jnp.mean(xf, axis=-1, keepdims=True)
    xc = xf - mu
    y = xc * lax.rsqrt(jnp.mean(xc * xc, axis=-1, keepdims=True) + EPS)
    return (y * g.astype(jnp.float32) + b.astype(jnp.float32)).astype(x.dtype)


def causal_depthwise_conv(x, w):
    K, C = w.shape
    return lax.conv_general_dilated(
        x, w[:, None, :].astype(x.dtype), window_strides=(1,), padding=[(K - 1, 0)],
        dimension_numbers=("NWC", "WIO", "NWC"), feature_group_count=C)


def conformer_conv(u, w_dw, b_dw, ln_g, ln_b):
    a, g = jnp.split(u, 2, axis=-1)
    h = a * jax.nn.sigmoid(g)
    h = causal_depthwise_conv(h, w_dw) + b_dw
    h = layer_norm(h, ln_g, ln_b)
    return jax.nn.silu(h)


def forgetting_attention(q, k, v, f_logit):
    B, S = q.shape[0], q.shape[1]
    q = q.reshape(B, S, N_HEADS, HEAD_DIM)
    k = k.reshape(B, S, N_HEADS, HEAD_DIM)
    v = v.reshape(B, S, N_HEADS, HEAD_DIM)
    log_f = jax.nn.log_sigmoid(f_logit.astype(jnp.float32))
    c = jnp.cumsum(log_f, axis=1).transpose(0, 2, 1)
    nb = S // Q_BLOCK
    qb = q.reshape(B, nb, Q_BLOCK, N_HEADS, HEAD_DIM).transpose(1, 0, 2, 3, 4)
    cqb = c.reshape(B, N_HEADS, nb, Q_BLOCK).transpose(2, 0, 1, 3)
    k_pos = jnp.arange(S)
    scale = HEAD_DIM ** -0.5

    def one_block(args):
        qi, cqi, i = args
        s = jnp.einsum("bqhd,bkhd->bhqk", qi, k, preferred_element_type=jnp.float32) * scale
        s = s + cqi[:, :, :, None] - c[:, :, None, :]
        q_pos = i * Q_BLOCK + jnp.arange(Q_BLOCK)
        mask = k_pos[None, :] <= q_pos[:, None]
        s = jnp.where(mask[None, None], s, -jnp.inf)
        p = jax.nn.softmax(s, axis=-1)
        return jnp.einsum("bhqk,bkhd->bqhd", p.astype(v.dtype), v)

    o = lax.map(one_block, (qb, cqb, jnp.arange(nb, dtype=jnp.int32)))
    return o.transpose(1, 0, 2, 3, 4).reshape(B, S, ATTN_W)


def setup_inputs(seed: int = 0) -> dict:
    key = jax.random.key(seed)
    ks = jax.random.split(key, 16)
    f32 = jnp.float32

    def nrm(k, shape, scale):
        return jax.random.normal(k, shape, f32) * scale

    def gain(k, shape):
        return 1.0 + 0.05 * jax.random.normal(k, shape, f32)

    return {
        "x": jax.random.normal(ks[0], (BATCH, SEQ, D_MODEL), f32),
        "pre_mix_g": gain(ks[1], (DEPTH, D_MODEL)),
        "w_in": nrm(ks[2], (DEPTH, D_MODEL, IN_COLS), D_MODEL ** -0.5),
        "b_forget": jax.random.uniform(ks[3], (DEPTH, N_HEADS), f32, 1.0, 6.0),
        "conv_w": nrm(ks[4], (DEPTH, CONV_K, CONV_CH), CONV_K ** -0.5),
        "conv_b": nrm(ks[5], (DEPTH, CONV_CH), 0.02),
        "conv_ln_g": gain(ks[6], (DEPTH, CONV_CH)),
        "conv_ln_b": nrm(ks[7], (DEPTH, CONV_CH), 0.02),
        "w_out": nrm(ks[8], (DEPTH, D_MIX, D_MODEL), D_MIX ** -0.5),
        "post_mix_g": gain(ks[9], (DEPTH, D_MODEL)),
        "pre_ffn_g": gain(ks[10], (DEPTH, D_MODEL)),
        "w_up": nrm(ks[11], (DEPTH, D_MODEL, 2 * D_FF), D_MODEL ** -0.5),
        "ffn_conv_w": nrm(ks[12], (DEPTH, FFN_CONV_K, 2 * D_FF), FFN_CONV_K ** -0.5),
        "w_down": nrm(ks[13], (DEPTH, D_FF, D_MODEL), D_FF ** -0.5),
        "post_ffn_g": gain(ks[14], (DEPTH, D_MODEL)),
    }


def reference(x, pre_mix_g, w_in, b_forget, conv_w, conv_b, conv_ln_g, conv_ln_b,
              w_out, post_mix_g, pre_ffn_g, w_up, ffn_conv_w, w_down, post_ffn_g):
    c0 = 2 * CONV_CH
    for l in range(DEPTH):
        h = rms_norm(x, pre_mix_g[l])
        u = jnp.einsum("bsd,de->bse", h, w_in[l])
        u_conv = u[..., :c0]
        q = u[..., c0:c0 + ATTN_W]
        k = u[..., c0 + ATTN_W:c0 + 2 * ATTN_W]
        v = u[..., c0 + 2 * ATTN_W:c0 + 3 * ATTN_W]
        f_logit = u[..., c0 + 3 * ATTN_W:] + b_forget[l]
        a = conformer_conv(u_conv, conv_w[l], conv_b[l], conv_ln_g[l], conv_ln_b[l])
        b = forgetting_attention(q, k, v, f_logit)
        mix = jnp.einsum("bse,ed->bsd", jnp.concatenate([a, b], axis=-1), w_out[l])
        x = x + rms_norm(mix, post_mix_g[l])
        h = rms_norm(x, pre_ffn_g[l])
        z = causal_depthwise_conv(jnp.einsum("bsd,df->bsf", h, w_up[l]), ffn_conv_w[l])
        zg, zv = jnp.split(z, 2, axis=-1)
        y = jnp.einsum("bsf,fd->bsd", jax.nn.gelu(zg, approximate=True) * zv, w_down[l])
        x = x + rms_norm(y, post_ffn_g[l])
    return x
```

```python
import numpy as np
from contextlib import ExitStack
import concourse.bass as bass
import concourse.mybir as mybir
from concourse.bass_utils import run_bass_kernel_spmd

F32 = mybir.dt.float32
BF16 = mybir.dt.bfloat16
AF = mybir.ActivationFunctionType
ALU = mybir.AluOpType

D = 2048
S = 4096
DEPTH = 4
TT = 512
NT = S // TT
DC = D // 128
CCH = 1024
NH = 8
HD = 128
DFF = 5632
FC = DFF // 128
INC = 5128
CK = 31
EPS = 1e-6
GW = 256
NEG = -30000.0


class Actor:
    def __init__(self, name, kind):
        self.name = name
        self.kind = kind
        self.n = 0
        self.ops = []
        self.marked = set()
        self.waited = {}
        self.sem = None
        self.rank = {}
        self.nrank = 0
        self.last_compute = 0
        self.nobarrier = False


class Buf:
    def __init__(self, name, t, persistent=False, nobarrier=False):
        self.name = name
        self.t = t
        self.lw = None
        self.rd = {}
        self.lane = None
        self.persistent = persistent
        self.nobarrier = nobarrier


class Prog:
    def __init__(self, nc, es, nlanes):
        self.nc = nc
        self.PE = Actor("pe", "eng")
        self.ACT = Actor("act", "eng")
        self.DVE = Actor("dve", "eng")
        self.POOL = Actor("pool", "eng")
        self.SP = Actor("sp", "eng")
        self.engs = [self.PE, self.ACT, self.DVE, self.POOL, self.SP]
        for a in self.engs:
            a.sem = es.enter_context(nc.semaphore(a.name))
        self.free_lanes = []
        self.all_lanes = []
        for i in range(nlanes):
            ln = Actor("lane%d" % i, "lane")
            ln.sem = es.enter_context(nc.semaphore(ln.name))
            self.free_lanes.append(ln)
            self.all_lanes.append(ln)
        self.phase_lanes = []

    def _need(self, eng, deps):
        best = {}
        for (a, i) in deps:
            if a is eng and eng is self.PE:
                continue
            if i <= eng.waited.get(a, 0):
                continue
            if i > best.get(a, 0):
                best[a] = i
        waits = []
        for a, i in best.items():
            eng.waited[a] = i
            if a.kind == "eng":
                a.marked.add(i)
            waits.append((a, i))
        return waits

    def op(self, eng, fn, reads=(), writes=()):
        deps = []
        for b in reads:
            if b.lw is not None:
                deps.append(b.lw)
        for b in writes:
            if b.lw is not None:
                deps.append(b.lw)
            deps.extend(b.rd.items())
        waits = self._need(eng, deps)
        eng.n += 1
        idx = eng.n
        eng.last_compute = idx
        eng.ops.append((fn, waits, idx, None))
        for b in reads:
            b.rd[eng] = idx
        for b in writes:
            b.lw = (eng, idx)
            b.rd = {}

    def dma(self, q, fn, src, dst, chain=True):
        if dst.lane is None:
            dst.lane = self.free_lanes.pop()
            dst.lane.nobarrier = dst.nobarrier
            if not dst.persistent:
                self.phase_lanes.append(dst)
        lane = dst.lane
        deps = []
        if src.lw is not None:
            deps.append(src.lw)
        if dst.lw is not None and (chain or dst.lw[0] is not lane):
            deps.append(dst.lw)
        deps.extend(dst.rd.items())
        if chain and lane.n > 0:
            deps.append((lane, lane.n))
        waits = self._need(q, deps)
        lane.n += 1
        q.n += 1
        q.ops.append((fn, waits, q.n, lane))
        src.rd[lane] = lane.n
        dst.lw = (lane, lane.n)
        dst.rd = {}

    def barrier(self):
        deps = [(e, e.last_compute) for e in self.engs if e.last_compute > 0]
        deps += [(ln, ln.n) for ln in self.all_lanes if ln.n > 0 and not ln.nobarrier]
        for e in self.engs:
            waits = self._need(e, deps)
            e.n += 1
            e.ops.append((None, waits, e.n, None))

    def emit(self):
        self.barrier()
        nc = self.nc
        for a in self.engs:
            for i in sorted(a.marked):
                if i not in a.rank:
                    a.nrank += 1
                    a.rank[i] = a.nrank
        with nc.Block() as block:
            def val(a, i):
                return a.rank[i] if a.kind == "eng" else 16 * i

            def run(actor):
                def body(h):
                    for (fn, waits, idx, lane) in actor.ops:
                        for (a, i) in waits:
                            h.wait_ge(a.sem, val(a, i))
                        if fn is None:
                            continue
                        ins = fn(h)
                        if lane is not None:
                            ins.then_inc(lane.sem, 16)
                        elif idx in actor.marked:
                            ins.then_inc(actor.sem, 1)
                    actor.ops = []
                return body

            block.tensor(run(self.PE))
            block.scalar(run(self.ACT))
            block.vector(run(self.DVE))
            block.gpsimd(run(self.POOL))
            block.sync(run(self.SP))
        for a in self.engs:
            a.marked = set()
        for b in self.phase_lanes:
            b.lane.nobarrier = False
            self.free_lanes.append(b.lane)
            b.lane = None
        self.phase_lanes = []


def build_nc(layers, dbg=False, phases="ABCD"):
    nc = bass.Bass("TRN2", target_bir_lowering=False)
    L = len(layers)

    def din(name, shape):
        return nc.dram_tensor(name, shape, F32, kind="ExternalInput")

    x_in = din("x", [S, D])
    pre_mix_g = din("pre_mix_g", [L, D])
    w_in = din("w_in", [L, D, INC])
    b_forget = din("b_forget", [L, NH])
    conv_w = din("conv_w", [L, CK, CCH])
    conv_b = din("conv_b", [L, CCH])
    conv_ln_g = din("conv_ln_g", [L, CCH])
    conv_ln_b = din("conv_ln_b", [L, CCH])
    w_out = din("w_out", [L, D, D])
    post_mix_g = din("post_mix_g", [L, D])
    pre_ffn_g = din("pre_ffn_g", [L, D])
    w_up = din("w_up", [L, D, 2 * DFF])
    ffn_conv_w = din("ffn_conv_w", [L, 3, 2 * DFF])
    w_down = din("w_down", [L, DFF, D])
    post_ffn_g = din("post_ffn_g", [L, D])
    out = nc.dram_tensor("out", [S, D], F32, kind="ExternalOutput")

    skind = "ExternalOutput" if dbg else "Internal"
    XT = nc.dram_tensor("XT", [D, S], F32, kind=skind)
    QT = nc.dram_tensor("QT", [CCH, S], BF16, kind=skind)
    KT = nc.dram_tensor("KT", [CCH, S], BF16, kind=skind)
    VV = nc.dram_tensor("VV", [S, CCH], BF16, kind=skind)
    GT = nc.dram_tensor("GT", [CCH, S], BF16, kind=skind)
    MA = nc.dram_tensor("MA", [CCH, S], BF16, kind=skind)
    MB = nc.dram_tensor("MB", [CCH, S], BF16, kind=skind)
    CS = nc.dram_tensor("CS", [NH, 6, S], BF16, kind=skind)
    WINb = {}
    WFb = {}
    WOUTb = {}
    WUPb = {}
    WDNb = {}
    for l in layers:
        WINb[l] = nc.dram_tensor("WINb%d" % l, [20, 128, DC, GW], BF16)
        WFb[l] = nc.dram_tensor("WFb%d" % l, [128, DC, 8], BF16)
        WOUTb[l] = nc.dram_tensor("WOUTb%d" % l, [8, 128, DC, GW], BF16)
        WUPb[l] = nc.dram_tensor("WUPb%d" % l, [FC, 128, DC, GW], BF16)
        WDNb[l] = nc.dram_tensor("WDNb%d" % l, [DC, 128, FC, 128], BF16)

    with ExitStack() as es:
        P = Prog(nc, es, 92)
        PE, ACT, DVE, POOL, SP = P.PE, P.ACT, P.DVE, P.POOL, P.SP

        uid = [0]

        def sbt(scope, name, shape, dt, persistent=False):
            uid[0] += 1
            name = "%s_%d" % (name, uid[0])
            t = scope.enter_context(nc.sbuf_tensor(name, shape, dt))
            return Buf(name, t, persistent)

        def drb(name, t, nb=False):
            return Buf(name, t, True, nb)

        Bx = drb("x", x_in)
        Bout = drb("out", out)
        BXT = drb("XT", XT)
        BQT = drb("QT", QT)
        BKT = drb("KT", KT)
        BVV = drb("VV", VV)
        BGT = drb("GT", GT)
        BMA = drb("MA", MA)
        BMB = drb("MB", MB)
        BCS = drb("CS", CS)
        Bparam = drb("params", None)
        BWIN = {l: drb("WIN%d" % l, WINb[l], True) for l in layers}
        BWF = {l: drb("WF%d" % l, WFb[l], True) for l in layers}
        BWOUT = {l: drb("WOUT%d" % l, WOUTb[l], True) for l in layers}
        BWUP = {l: drb("WUP%d" % l, WUPb[l], True) for l in layers}
        BWDN = {l: drb("WDN%d" % l, WDNb[l], True) for l in layers}

        PS = []
        for i in range(8):
            t = es.enter_context(nc.psum_tensor("ps%d" % i, [128, 512], F32))
            PS.append(Buf("ps%d" % i, t, True))

        ones_bf = sbt(es, "ones_bf", [128, 128], BF16, True)
        ones_f = sbt(es, "ones_f", [128, 512], F32, True)
        ident_f = sbt(es, "ident_f", [128, 128], F32, True)
        ident_bf = sbt(es, "ident_bf", [128, 128], BF16, True)
        gv = sbt(es, "gv", [128, 4 * 64], F32, True)
        cv = sbt(es, "cv", [128, 3 * 32], F32, True)
        cw = sbt(es, "cw", [128, 1024], F32, True)
        fw = sbt(es, "fw", [128, 1152], F32, True)
        negb = sbt(es, "negb", [128, 4], F32, True)
        lfh = {}
        halo = sbt(es, "halo", [128, 2 * FC, 2], F32, True)

        def mm(ps_ap, lhsT, rhs, start, stop, reads, writes):
            P.op(PE, lambda e: e.matmul(ps_ap, lhsT=lhsT, rhs=rhs, start=start, stop=stop), reads, writes)

        def act(out_ap, in_ap, func, reads, writes, scale=1.0, bias=0.0):
            P.op(ACT, lambda e: e.activation(out=out_ap, in_=in_ap, func=func, bias=bias, scale=scale), reads, writes)

        def tcopy(eng, out_ap, in_ap, reads, writes):
            P.op(eng, lambda e: e.tensor_copy(out=out_ap, in_=in_ap), reads, writes)

        def tt(eng, out_ap, a, b, op, reads, writes):
            P.op(eng, lambda e: e.tensor_tensor(out=out_ap, in0=a, in1=b, op=op), reads, writes)

        def tsc(eng, out_ap, a, s1, s2, op0, op1, reads, writes):
            if op1 is None:
                P.op(eng, lambda e: e.tensor_scalar(out=out_ap, in0=a, scalar1=s1, scalar2=None, op0=op0), reads, writes)
            else:
                P.op(eng, lambda e: e.tensor_scalar(out=out_ap, in0=a, scalar1=s1, scalar2=s2, op0=op0, op1=op1), reads, writes)

        def stt(eng, out_ap, a, s, b, op0, op1, reads, writes):
            P.op(eng, lambda e: e.scalar_tensor_tensor(out=out_ap, in0=a, scalar=s, in1=b, op0=op0, op1=op1), reads, writes)

        def dma(q, out_ap, in_ap, src, dst, chain=True, **kw):
            P.dma(q, lambda e: e.dma_start(out=out_ap, in_=in_ap, **kw), src, dst, chain=chain)

        def prep_units(l):
            units = []
            for g in range(20):
                units.append((WINb[l][g], w_in[l, :, g * GW:(g + 1) * GW].rearrange("(c p) n -> p c n", p=128), BWIN[l]))
            units.append((WFb[l][:, :, :], w_in[l, :, 5120:5128].rearrange("(c p) n -> p c n", p=128), BWF[l]))
            for g in range(8):
                units.append((WOUTb[l][g], w_out[l, :, g * GW:(g + 1) * GW].rearrange("(c p) n -> p c n", p=128), BWOUT[l]))
            for g in range(FC):
                units.append((WUPb[l][g, :, :, 0:128], w_up[l, :, g * 128:(g + 1) * 128].rearrange("(c p) n -> p c n", p=128), BWUP[l]))
                units.append((WUPb[l][g, :, :, 128:256], w_up[l, :, DFF + g * 128:DFF + (g + 1) * 128].rearrange("(c p) n -> p c n", p=128), BWUP[l]))
            for g in range(DC):
                units.append((WDNb[l][g], w_down[l, :, g * 128:(g + 1) * 128].rearrange("(c p) n -> p c n", p=128), BWDN[l]))
            return units

        prep_q = []

        def prep_pull(n):
            for _ in range(n):
                if not prep_q:
                    return
                o, i, b = prep_q.pop(0)
                dma(POOL, o, i, Bparam, b, chain=True, max_dma_last_dim=2048)

        def phase_S():
            with ExitStack() as sc:
                ld = [sbt(sc, "ld%d" % i, [128, 128], F32) for i in range(2)]
                P.op(POOL, lambda e: e.memset(ones_f.t[:, :], 1.0), [], [ones_f])
                P.op(DVE, lambda e: e.memset(ones_bf.t[:, :], 1.0), [], [ones_bf])
                P.op(POOL, lambda e: e.memset(ident_f.t[:, :], 1.0), [], [ident_f])
                P.op(POOL, lambda e: e.affine_select(out=ident_f.t[:, :], in_=ident_f.t[:, :], pattern=[[-1, 128]],
                                                      compare_op=ALU.is_equal, fill=0.0, base=0, channel_multiplier=1),
                     [ident_f], [ident_f])
                tcopy(DVE, ident_bf.t[:, :], ident_f.t[:, :], [ident_f], [ident_bf])
                P.op(DVE, lambda e: e.memset(halo.t[:, :, :], 0.0), [], [halo])
                P.op(DVE, lambda e: e.memset(ld[0].t[:, :], 0.0), [], [ld[0]])
                P.op(DVE, lambda e: e.memset(ld[1].t[:, :], 0.0), [], [ld[1]])
                cnt = [0]

                def rows_T(src_ap, nrows, ncols, dst_buf, dst_ap, neg=False):
                    k = cnt[0] % 2
                    cnt[0] += 1
                    dma(SP, ld[k].t[0:nrows, 0:ncols], src_ap, Bparam, ld[k])
                    ps = PS[k]
                    P.op(PE, lambda e: e.transpose(ps.t[:, 0:128], ld[k].t[:, :], ident_f.t[:, :]), [ld[k], ident_f], [ps])
                    if neg:
                        tsc(DVE, dst_ap, ps.t[0:ncols, 0:nrows], -1.0, None, ALU.mult, None, [ps], [dst_buf])
                    else:
                        tcopy(DVE, dst_ap, ps.t[0:ncols, 0:nrows], [ps], [dst_buf])

                for kind, g in enumerate([pre_mix_g, post_mix_g, pre_ffn_g, post_ffn_g]):
                    rows_T(g.ap().rearrange("l (c p) -> (l c) p", p=128), 16 * L, 128, gv, gv.t[:, kind * 64:kind * 64 + 16 * L])
                for kind, g in enumerate([conv_b, conv_ln_g, conv_ln_b]):
                    rows_T(g.ap().rearrange("l (c p) -> (l c) p", p=128), 8 * L, 128, cv, cv.t[:, kind * 32:kind * 32 + 8 * L])
                cwv = conv_w.ap().rearrange("l k (c p) -> (l k c) p", p=128)
                for r0 in range(0, L * CK * 8, 128):
                    n = min(128, L * CK * 8 - r0)
                    rows_T(cwv[r0:r0 + n, :], n, 128, cw, cw.t[:, r0:r0 + n])
                fwv = ffn_conv_w.ap().rearrange("l k (c p) -> (l k c) p", p=128)
                for r0 in range(0, L * 3 * 88, 128):
                    n = min(128, L * 3 * 88 - r0)
                    rows_T(fwv[r0:r0 + n, :], n, 128, fw, fw.t[:, r0:r0 + n])
                rows_T(b_forget.ap(), L, 8, negb, negb.t[0:8, 0:L], neg=True)
                P.emit()

        def gcol(kind, l, c):
            return gv.t[:, kind * 64 + l * 16 + c:kind * 64 + l * 16 + c + 1]

        def phase_T():
            with ExitStack() as sc:
                xin = [sbt(sc, "xin%d" % i, [128, D], F32) for i in range(4)]
                st = [sbt(sc, "xst%d" % i, [128, DC, TT], F32) for i in range(2)]
                XTv = XT.ap().rearrange("(c p) t -> p c t", p=128)
                k = 0
                for ti in range(NT):
                    sb = st[ti % 2]
                    for s in range(4):
                        xb = xin[k % 4]
                        r0 = ti * TT + s * 128
                        dma(SP, xb.t[:, :], x_in[r0:r0 + 128, :], Bx, xb)
                        for c4 in range(4):
                            ps = PS[(k * 4 + c4) % 8]
                            for j in range(4):
                                c = c4 * 4 + j
                                P.op(PE, (lambda e, ps=ps, xb=xb, c=c, j=j: e.transpose(
                                    ps.t[:, j * 128:(j + 1) * 128], xb.t[:, c * 128:(c + 1) * 128], ident_f.t[:, :])),
                                     [xb, ident_f], [ps])
                            eng = ACT if c4 % 2 == 0 else DVE
                            o = sb.t[:, c4 * 4:(c4 + 1) * 4, s * 128:(s + 1) * 128]
                            i_ = ps.t[:, :].rearrange("p (j t) -> p j t", j=4)
                            if eng is ACT:
                                P.op(ACT, (lambda e, o=o, i_=i_: e.copy(out=o, in_=i_)), [ps], [sb])
                            else:
                                tcopy(DVE, o, i_, [ps], [sb])
                        k += 1
                    dma(ACT, XTv[:, :, ti * TT:(ti + 1) * TT], sb.t[:, :, :], sb, BXT)
                    prep_pull(6)
                P.emit()

        def phase_U():
            with ExitStack() as sc:
                xt = [sbt(sc, "uxt%d" % i, [128, DC, TT], F32) for i in range(2)]
                os_ = [sbt(sc, "uos%d" % i, [128, D], F32) for i in range(2)]
                XTv = XT.ap().rearrange("(c p) t -> p c t", p=128)
                k = 0
                for ti in range(NT):
                    xb = xt[ti % 2]
                    dma(SP, xb.t[:, :, :], XTv[:, :, ti * TT:(ti + 1) * TT], BXT, xb)
                    for s in range(4):
                        ob = os_[k % 2]
                        for c4 in range(4):
                            ps = PS[(k * 4 + c4) % 8]
                            for j in range(4):
                                c = c4 * 4 + j
                                P.op(PE, (lambda e, ps=ps, xb=xb, c=c, j=j, s=s: e.transpose(
                                    ps.t[:, j * 128:(j + 1) * 128], xb.t[:, c, s * 128:(s + 1) * 128], ident_f.t[:, :])),
                                     [xb, ident_f], [ps])
                            o = ob.t[:, c4 * 512:(c4 + 1) * 512]
                            if c4 % 2 == 0:
                                P.op(ACT, (lambda e, o=o, ps=ps: e.copy(out=o, in_=ps.t[:, :])), [ps], [ob])
                            else:
                                tcopy(DVE, o, ps.t[:, :], [ps], [ob])
                        r0 = ti * TT + s * 128
                        dma(ACT, out[r0:r0 + 128, :], ob.t[:, :], ob, Bout)
                        k += 1
                P.emit()

        def rms_r(psb, sq, rtmp, r, src_buf, src_chunk, nchunks, n_feat):
            for c in range(nchunks):
                s = sq[c % len(sq)]
                act(s.t[:, :], src_chunk(c), AF.Square, [src_buf], [s])
                mm(psb.t[:, :], ones_bf.t[:, :], s.t[:, :], c == 0, c == nchunks - 1, [ones_bf, s], [psb])
            act(rtmp.t[:, :], psb.t[:, :], AF.Sqrt, [psb], [rtmp], scale=1.0 / n_feat, bias=EPS)
            P.op(DVE, lambda e: e.reciprocal(out=r.t[:, :], in_=rtmp.t[:, :]), [rtmp], [r])

        def phase_A(l):
            lf_all = lfh["lf"]
            with ExitStack() as sc:
                xTs = [sbt(sc, "a_xT%d" % i, [128, DC, TT], F32) for i in range(2)]
                hTs = [sbt(sc, "a_hT%d" % i, [128, DC, TT], BF16) for i in range(2)]
                wsl = [sbt(sc, "a_w%d" % i, [128, DC, GW], BF16) for i in range(3)]
                wf = sbt(sc, "a_wf", [128, DC, 8], BF16)
                sq = [sbt(sc, "a_sq%d" % i, [128, TT], BF16) for i in range(2)]
                rtmp = sbt(sc, "a_rtmp", [128, TT], F32)
                r = sbt(sc, "a_r", [128, TT], F32)
                a_st = sbt(sc, "a_ast", [128, 8, TT], BF16)
                g_st = sbt(sc, "a_gst", [128, 8, TT], BF16)
                q_st = sbt(sc, "a_qst", [128, 8, TT], BF16)
                k_st = sbt(sc, "a_kst", [128, 8, TT], BF16)
                v_st = sbt(sc, "a_vst", [128, 4, CCH], BF16)
                sig = [sbt(sc, "a_sig%d" % i, [128, TT], F32) for i in range(2)]
                e1 = sbt(sc, "a_e1", [8, TT], F32)
                XTv = XT.ap().rearrange("(c p) t -> p c t", p=128)
                dma(SP, wf.t[:, :, :], WFb[l][:, :, :], BWF[l], wf)
                seq = [(ti, g) for ti in range(NT) for g in range(20)]
                state = {"next": 0}

                def wload():
                    i = state["next"]
                    if i >= len(seq):
                        return
                    g = seq[i][1]
                    dma(SP, wsl[i % 3].t[:, :, :], WINb[l][g], BWIN[l], wsl[i % 3])
                    state["next"] += 1

                psr = PS[0]
                psf = PS[1]
                pm = PS[2:8]
                pmi = [0]

                def nextps():
                    p = pm[pmi[0] % 6]
                    pmi[0] += 1
                    return p

                widx = 0
                dma(SP, xTs[0].t[:, :, :], XTv[:, :, 0:TT], BXT, xTs[0])

                def norm(tj):
                    xT_ = xTs[tj % 2]
                    hT_ = hTs[tj % 2]
                    rms_r(psr, sq, rtmp, r, xT_, lambda c, xT_=xT_: xT_.t[:, c, :], DC, D)
                    for c in range(DC):
                        stt(DVE, hT_.t[:, c, :], xT_.t[:, c, :], gcol(0, l, c), r.t[:, :], ALU.mult, ALU.mult, [xT_, gv, r], [hT_])

                wload()
                wload()
                norm(0)
                for ti in range(NT):
                    t0 = ti * TT
                    hT = hTs[ti % 2]
                    if ti + 1 < NT:
                        dma(SP, xTs[(ti + 1) % 2].t[:, :, :], XTv[:, :, t0 + TT:t0 + 2 * TT], BXT, xTs[(ti + 1) % 2])
                    for g in range(20):
                        if g == 12 and ti + 1 < NT:
                            norm(ti + 1)
                        w = wsl[widx % 3]
                        wload()
                        widx += 1
                        if g < 16:
                            for j in range(2):
                                ch = g * 2 + j
                                ps = nextps()
                                for c in range(DC):
                                    mm(ps.t[:, :], w.t[:, c, j * 128:(j + 1) * 128], hT.t[:, c, :], c == 0, c == DC - 1, [w, hT], [ps])
                                if ch < 8:
                                    tcopy(DVE, a_st.t[:, ch, :], ps.t[:, :], [ps], [a_st])
                                elif ch < 16:
                                    sg = sig[ch % 2]
                                    act(sg.t[:, :], ps.t[:, :], AF.Sigmoid, [ps], [sg])
                                    tt(DVE, g_st.t[:, ch - 8, :], a_st.t[:, ch - 8, :], sg.t[:, :], ALU.mult, [a_st, sg], [g_st])
                                elif ch < 24:
                                    act(q_st.t[:, ch - 16, :], ps.t[:, :], AF.Copy, [ps], [q_st], scale=float(HD) ** -0.5)
                                else:
                                    tcopy(DVE, k_st.t[:, ch - 24, :], ps.t[:, :], [ps], [k_st])
                        else:
                            for s in range(4):
                                ps = nextps()
                                for c in range(DC):
                                    mm(ps.t[:, 0:GW], hT.t[:, c, s * 128:(s + 1) * 128], w.t[:, c, :], c == 0, c == DC - 1, [w, hT], [ps])
                                o = v_st.t[:, s, (g - 16) * GW:(g - 15) * GW]
                                if s % 2 == 0:
                                    P.op(ACT, (lambda e, o=o, ps=ps: e.copy(out=o, in_=ps.t[:, 0:GW])), [ps], [v_st])
                                else:
                                    tcopy(DVE, o, ps.t[:, 0:GW], [ps], [v_st])
                        if g == 7:
                            dma(POOL, GT.ap().rearrange("(c p) t -> p c t", p=128)[:, :, t0:t0 + TT], g_st.t[:, :, :], g_st, BGT)
                        if g == 11:
                            dma(POOL, QT.ap().rearrange("(c p) t -> p c t", p=128)[:, :, t0:t0 + TT], q_st.t[:, :, :], q_st, BQT)
                        if g == 15:
                            dma(POOL, KT.ap().rearrange("(c p) t -> p c t", p=128)[:, :, t0:t0 + TT], k_st.t[:, :, :], k_st, BKT)
                    dma(POOL, VV.ap().rearrange("(s p) e -> p s e", p=128)[:, ti * 4:(ti + 1) * 4, :], v_st.t[:, :, :], v_st, BVV)
                    for c in range(DC):
                        mm(psf.t[0:8, :], wf.t[:, c, :], hT.t[:, c, :], c == 0, c == DC - 1, [wf, hT], [psf])
                    act(e1.t[:, :], psf.t[0:8, :], AF.Exp, [psf, negb], [e1], scale=-1.0, bias=negb.t[0:8, l:l + 1])
                    act(lf_all.t[:, t0:t0 + TT], e1.t[:, :], AF.Ln, [e1], [lf_all], scale=1.0, bias=1.0)
                P.emit()

        def phase_B(l):
            lf_all = lfh["lf"]
            with ExitStack() as sc:
                cs = sbt(sc, "b_cs", [8, S], F32)
                r1 = sbt(sc, "b_r1", [8, S], F32)
                pcs = sbt(sc, "b_pcs", [8, 6, S], BF16)
                for i in range(NT):
                    init = 0.0 if i == 0 else cs.t[:, i * TT - 1:i * TT]
                    P.op(DVE, (lambda e, i=i, init=init: e.tensor_tensor_scan(
                        out=cs.t[:, i * TT:(i + 1) * TT], data0=ones_f.t[0:8, :], data1=lf_all.t[:, i * TT:(i + 1) * TT],
                        initial=init, op0=ALU.mult, op1=ALU.add)), [ones_f, lf_all, cs], [cs])
                tsc(DVE, pcs.t[:, 0, :], cs.t[:, :], -1.0, None, ALU.mult, None, [cs], [pcs])
                stt(DVE, r1.t[:, :], cs.t[:, :], -1.0, pcs.t[:, 0, :], ALU.mult, ALU.subtract, [cs, pcs], [r1])
                tcopy(DVE, pcs.t[:, 1, :], r1.t[:, :], [r1], [pcs])
                tt(DVE, cs.t[:, :], r1.t[:, :], pcs.t[:, 1, :], ALU.subtract, [r1, pcs], [cs])
                tcopy(DVE, pcs.t[:, 2, :], cs.t[:, :], [cs], [pcs])
                for j in range(3):
                    tsc(DVE, pcs.t[:, 3 + j, :], pcs.t[:, j, :], -1.0, None, ALU.mult, None, [pcs], [pcs])
                dma(SP, CS.ap(), pcs.t[:, :, :], pcs, BCS)
                P.emit()

        def phase_C(l):
            with ExitStack() as sc:
                qT = [sbt(sc, "c_q%d" % i, [128, S], BF16) for i in range(2)]
                kT = [sbt(sc, "c_k%d" % i, [128, S], BF16) for i in range(2)]
                vv = [sbt(sc, "c_v%d" % i, [128, 32, HD], BF16) for i in range(2)]
                bl = [sbt(sc, "c_bl%d" % i, [128, S], BF16) for i in range(2)]
                br = [sbt(sc, "c_br%d" % i, [128, S], BF16) for i in range(2)]
                pT = [sbt(sc, "c_p%d" % i, [128, TT], BF16) for i in range(3)]
                ost = [sbt(sc, "c_o%d" % i, [128, TT], BF16) for i in range(3)]
                rl = [sbt(sc, "c_rl%d" % i, [128, TT], F32) for i in range(2)]
                gl = [sbt(sc, "c_gl%d" % i, [128, 8, TT + 32], BF16) for i in range(2)]
                acc = sbt(sc, "c_acc", [128, 8, TT], F32)
                csq = sbt(sc, "c_sq", [128, TT], F32)
                mean = sbt(sc, "c_mean", [128, TT], F32)
                msq = sbt(sc, "c_msq", [128, TT], F32)
                var = sbt(sc, "c_var", [128, TT], F32)
                rstd = sbt(sc, "c_rstd", [128, TT], F32)
                tmp = [sbt(sc, "c_tmp%d" % i, [128, TT], F32) for i in range(2)]
                a_st = sbt(sc, "c_ast", [128, 8, TT], BF16)
                maskb = sbt(sc, "c_maskb", [128, 4, 512], BF16)
                mk = sbt(sc, "c_mk", [128, 4, 512], F32)
                P.op(POOL, lambda e: e.memset(mk.t[:, :, :], 0.0), [], [mk])
                for i in range(4):
                    P.op(POOL, (lambda e, i=i: e.affine_select(out=mk.t[:, i, :], in_=mk.t[:, i, :], pattern=[[1, 512]],
                                                                compare_op=ALU.is_ge, fill=NEG, base=-128 * i,
                                                                channel_multiplier=-1)), [mk], [mk])
                tcopy(DVE, maskb.t[:, :, :], mk.t[:, :, :], [mk], [maskb])
                for i in range(2):
                    P.op(DVE, (lambda e, i=i: e.memset(bl[i].t[:, :], 0.0)), [], [bl[i]])
                    P.op(DVE, (lambda e, i=i: e.memset(br[i].t[:, :], 0.0)), [], [br[i]])
                    P.op(DVE, (lambda e, i=i: e.memset(bl[i].t[0:6, :], 1.0)), [], [bl[i]])
                    P.op(DVE, (lambda e, i=i: e.memset(br[i].t[0:6, :], 1.0)), [], [br[i]])
                    P.op(DVE, (lambda e, i=i: e.memset(gl[i].t[:, :, :], 0.0)), [], [gl[i]])
                pst = PS[6]
                pcv = PS[6]
                dg = [sbt(sc, "c_dg%d" % i, [128, CK, 128], BF16) for i in range(2)]
                GTv = GT.ap().rearrange("(c p) t -> p c t", p=128)
                MAv = MA.ap().rearrange("(c p) t -> p c t", p=128)

                def conv_gen():
                    HAL = 32
                    cwl = cw.t[:, l * CK * 8:(l + 1) * CK * 8].rearrange("p (k c) -> p k c", c=8)

                    def build(n):
                        cc_ = n % 8
                        dgt_ = dg[n % 2]
                        tt(DVE, dgt_.t[:, :, :], ident_bf.t[:, :].unsqueeze(1).broadcast_to([128, CK, 128]),
                           cwl[:, :, cc_].unsqueeze(2).broadcast_to([128, CK, 128]), ALU.mult, [ident_bf, cw], [dgt_])

                    build(0)
                    for ti in range(NT):
                        g = gl[ti % 2]
                        t0 = ti * TT
                        if ti == 0:
                            dma(SP, g.t[:, :, HAL:HAL + TT], GTv[:, :, 0:TT], BGT, g)
                        else:
                            dma(SP, g.t[:, :, 0:HAL + TT], GTv[:, :, t0 - HAL:t0 + TT], BGT, g)
                        yield
                        for cc in range(8):
                            n_ = ti * 8 + cc
                            dgt = dg[n_ % 2]
                            if n_ + 1 < NT * 8:
                                build(n_ + 1)
                            for k in range(CK):
                                mm(pcv.t[:, :], dgt.t[:, k, :], g.t[:, cc, 2 + k:2 + k + TT], k == 0, k == CK - 1, [dgt, g], [pcv])
                            act(acc.t[:, cc, :], pcv.t[:, :], AF.Identity, [pcv, cv], [acc], scale=1.0,
                                bias=cv.t[:, l * 8 + cc:l * 8 + cc + 1])
                            yield
                        for cc in range(8):
                            mm(pst.t[:, :], ones_f.t[:, 0:128], acc.t[:, cc, :], cc == 0, cc == 7, [ones_f, acc], [pst])
                        act(mean.t[:, :], pst.t[:, :], AF.Copy, [pst], [mean], scale=1.0 / CCH)
                        act(msq.t[:, :], pst.t[:, :], AF.Square, [pst], [msq], scale=1.0 / CCH)
                        for cc in range(8):
                            act(csq.t[:, :], acc.t[:, cc, :], AF.Square, [acc], [csq])
                            mm(pst.t[:, :], ones_f.t[:, 0:128], csq.t[:, :], cc == 0, cc == 7, [ones_f, csq], [pst])
                        yield
                        stt(DVE, var.t[:, :], pst.t[:, :], 1.0 / CCH, msq.t[:, :], ALU.mult, ALU.subtract, [pst, msq], [var])
                        act(var.t[:, :], var.t[:, :], AF.Sqrt, [var], [var], scale=1.0, bias=EPS)
                        P.op(DVE, lambda e: e.reciprocal(out=rstd.t[:, :], in_=var.t[:, :]), [var], [rstd])
                        for cc in range(8):
                            tm = tmp[cc % 2]
                            tt(DVE, tm.t[:, :], acc.t[:, cc, :], mean.t[:, :], ALU.subtract, [acc, mean], [tm])
                            tt(DVE, tm.t[:, :], tm.t[:, :], rstd.t[:, :], ALU.mult, [tm, rstd], [tm])
                            act(a_st.t[:, cc, :], tm.t[:, :], AF.Silu, [tm, cv], [a_st],
                                scale=cv.t[:, 32 + l * 8 + cc:32 + l * 8 + cc + 1], bias=cv.t[:, 64 + l * 8 + cc:64 + l * 8 + cc + 1])
                        dma(POOL, MAv[:, :, t0:t0 + TT], a_st.t[:, :, :], a_st, BMA)
                        yield
                    while True:
                        yield

                cg = conv_gen()
                n_units = NT * (1 + 8 + 2)
                pulled = [0]

                def conv_pull(target):
                    while pulled[0] < target:
                        next(cg)
                        pulled[0] += 1

                def head_load(h):
                    sl = h % 2
                    dma(SP, qT[sl].t[:, :], QT[h * 128:(h + 1) * 128, :], BQT, qT[sl])
                    dma(SP, kT[sl].t[:, :], KT[h * 128:(h + 1) * 128, :], BKT, kT[sl])
                    dma(SP, vv[sl].t[:, :, :], VV.ap().rearrange("(b p) e -> p b e", p=128)[:, :, h * HD:(h + 1) * HD], BVV, vv[sl])
                    dma(SP, br[sl].t[0:3, :], CS[h, 0:3, :], BCS, br[sl])
                    dma(SP, bl[sl].t[3:6, :], CS[h, 3:6, :], BCS, bl[sl])

                head_load(0)
                qtc = 0
                total_blocks = NH * sum(4 * j + 4 for j in range(NT))
                done_blocks = 0
                for h in range(NH):
                    sl = h % 2
                    if h + 1 < NH:
                        head_load(h + 1)
                    for j in range(NT):
                        q0 = j * TT
                        nb = 4 * j + 4
                        po = PS[3 + (qtc % 2)]
                        pl = PS[5] if qtc % 2 == 0 else PS[7]

                        def QK(i):
                            ps = PS[i % 3]
                            diag = i >= 4 * j
                            mm(ps.t[:, :], kT[sl].t[:, i * 128:(i + 1) * 128], qT[sl].t[:, q0:q0 + TT], True, False, [kT[sl], qT[sl]], [ps])
                            mm(ps.t[:, :], bl[sl].t[:, i * 128:(i + 1) * 128], br[sl].t[:, q0:q0 + TT], False, not diag, [bl[sl], br[sl]], [ps])
                            if diag:
                                mm(ps.t[:, :], ident_bf.t[:, :], maskb.t[:, i - 4 * j, :], False, True, [ident_bf, maskb], [ps])

                        def PV(i):
                            p = pT[i % 3]
                            act(p.t[:, :], PS[i % 3].t[:, :], AF.Exp, [PS[i % 3]], [p])
                            mm(po.t[:, :], vv[sl].t[:, i, :], p.t[:, :], i == 0, i == nb - 1, [vv[sl], p], [po])
                            mm(pl.t[:, :], ones_bf.t[:, :], p.t[:, :], i == 0, i == nb - 1, [ones_bf, p], [pl])

                        QK(0)
                        if nb > 1:
                            QK(1)
                        for i in range(nb):
                            if i + 2 < nb:
                                QK(i + 2)
                            PV(i)
                        rr = rl[qtc % 2]
                        oo = ost[qtc % 3]
                        P.op(DVE, (lambda e, rr=rr, pl=pl: e.reciprocal(out=rr.t[:, :], in_=pl.t[:, :])), [pl], [rr])
                        tt(DVE, oo.t[:, :], po.t[:, :], rr.t[:, :], ALU.mult, [po, rr], [oo])
                        dma(POOL, MB[h * 128:(h + 1) * 128, q0:q0 + TT], oo.t[:, :], oo, BMB)
                        qtc += 1
                        done_blocks += nb
                        conv_pull(int(n_units * done_blocks / total_blocks) + 1)
                conv_pull(n_units + 4)
                P.emit()

        def phase_D(l, last):
            with ExitStack() as sc:
                xT = sbt(sc, "d_xT", [128, DC, TT], F32)
                hT = sbt(sc, "d_hT", [128, DC, TT], BF16)
                aT = sbt(sc, "d_aT", [128, FC, TT], BF16)
                ys = sbt(sc, "d_ys", [128, DC, TT], F32)
                ysc = [Buf("ysc%d" % i, ys.t) for i in range(DC)]
                wsl = [sbt(sc, "d_w%d" % i, [128, DC, GW], BF16) for i in range(3)]
                wdn = [sbt(sc, "d_wd%d" % i, [128, FC // 2, 128], BF16) for i in range(4)]
                sq = [sbt(sc, "d_sq%d" % i, [128, TT], BF16) for i in range(2)]
                rtmp = sbt(sc, "d_rtmp", [128, TT], F32)
                r = sbt(sc, "d_r", [128, TT], F32)
                yb = [sbt(sc, "d_yb%d" % i, [128, TT + 2], F32) for i in range(2)]
                zc = [sbt(sc, "d_zc%d" % i, [128, TT], F32) for i in range(4)]
                P.op(DVE, lambda e: e.memset(halo.t[:, :, :], 0.0), [], [halo])
                XTv = XT.ap().rearrange("(c p) t -> p c t", p=128)
                MAv = MA.ap().rearrange("(c p) t -> p c t", p=128)
                MBv = MB.ap().rearrange("(c p) t -> p c t", p=128)
                seq = []
                for ti in range(NT):
                    seq += [("o", g) for g in range(8)] + [("u", g) for g in range(FC)]
                state = {"next": 0, "dnext": 0}

                def wload():
                    i = state["next"]
                    if i >= len(seq):
                        return
                    kind, g = seq[i]
                    if kind == "o":
                        dma(SP, wsl[i % 3].t[:, :, :], WOUTb[l][g], BWOUT[l], wsl[i % 3])
                    else:
                        dma(SP, wsl[i % 3].t[:, :, :], WUPb[l][g], BWUP[l], wsl[i % 3])
                    state["next"] += 1

                def dload():
                    i = state["dnext"]
                    if i >= NT * DC * 2:
                        return
                    dc_, hf = (i // 2) % DC, i % 2
                    dma(SP, wdn[i % 4].t[:, :, :], WDNb[l][dc_][:, hf * 22:(hf + 1) * 22, :], BWDN[l], wdn[i % 4])
                    state["dnext"] += 1

                gcnt = [0]

                def gtick():
                    gcnt[0] += 1
                    if gcnt[0] % 4 == 0:
                        prep_pull(1)

                psr = PS[0]
                pm = PS[1:8]
                pmi = [0]

                def nextps():
                    p = pm[pmi[0] % 7]
                    pmi[0] += 1
                    return p

                widx = 0
                didx = 0
                wload()
                wload()
                dload()
                dload()
                dload()
                for ti in range(NT):
                    t0 = ti * TT
                    dma(SP, hT.t[:, 0:8, :], MAv[:, :, t0:t0 + TT], BMA, hT)
                    dma(SP, hT.t[:, 8:16, :], MBv[:, :, t0:t0 + TT], BMB, hT)
                    chunk = 0
                    pend = None
                    for g in range(8):
                        w = wsl[widx % 3]
                        wload()
                        widx += 1
                        gtick()
                        for j in range(2):
                            ps = nextps()
                            for c in range(DC):
                                mm(ps.t[:, :], w.t[:, c, j * 128:(j + 1) * 128], hT.t[:, c, :], c == 0, c == DC - 1, [w, hT], [ps])
                            if pend is not None:
                                mm(psr.t[:, :], ones_bf.t[:, :], pend[0].t[:, :], pend[1] == 0, pend[1] == DC - 1, [ones_bf, pend[0]], [psr])
                            if chunk % 2 == 0:
                                P.op(ACT, (lambda e, ps=ps, chunk=chunk: e.copy(out=ys.t[:, chunk, :], in_=ps.t[:, :])), [ps], [ysc[chunk]])
                            else:
                                tcopy(DVE, ys.t[:, chunk, :], ps.t[:, :], [ps], [ysc[chunk]])
                            s = sq[chunk % 2]
                            act(s.t[:, :], ys.t[:, chunk, :], AF.Square, [ysc[chunk]], [s])
                            pend = (s, chunk)
                            chunk += 1
                    mm(psr.t[:, :], ones_bf.t[:, :], pend[0].t[:, :], pend[1] == 0, pend[1] == DC - 1, [ones_bf, pend[0]], [psr])
                    dma(SP, xT.t[:, :, :], XTv[:, :, t0:t0 + TT], BXT, xT)
                    act(rtmp.t[:, :], psr.t[:, :], AF.Sqrt, [psr], [rtmp], scale=1.0 / D, bias=EPS)
                    P.op(DVE, lambda e: e.reciprocal(out=r.t[:, :], in_=rtmp.t[:, :]), [rtmp], [r])
                    for c in range(DC):
                        stt(DVE, ys.t[:, c, :], ys.t[:, c, :], gcol(1, l, c), r.t[:, :], ALU.mult, ALU.mult, [ysc[c], gv, r], [ysc[c]])
                        tt(DVE, xT.t[:, c, :], xT.t[:, c, :], ys.t[:, c, :], ALU.add, [xT, ysc[c]], [xT])
                        s = sq[c % 2]
                        act(s.t[:, :], xT.t[:, c, :], AF.Square, [xT], [s])
                        mm(psr.t[:, :], ones_bf.t[:, :], s.t[:, :], c == 0, c == DC - 1, [ones_bf, s], [psr])
                    act(rtmp.t[:, :], psr.t[:, :], AF.Sqrt, [psr], [rtmp], scale=1.0 / D, bias=EPS)
                    P.op(DVE, lambda e: e.reciprocal(out=r.t[:, :], in_=rtmp.t[:, :]), [rtmp], [r])
                    for c in range(DC):
                        stt(DVE, hT.t[:, c, :], xT.t[:, c, :], gcol(2, l, c), r.t[:, :], ALU.mult, ALU.mult, [xT, gv, r], [hT])
                    for g in range(FC):
                        w = wsl[widx % 3]
                        wload()
                        widx += 1
                        gtick()
                        zz = []
                        for j in range(2):
                            fci = g + j * FC
                            ps = nextps()
                            for c in range(DC):
                                mm(ps.t[:, :], w.t[:, c, j * 128:(j + 1) * 128], hT.t[:, c, :], c == 0, c == DC - 1, [w, hT], [ps])
                            y = yb[(g * 2 + j) % 2]
                            z = zc[(g * 2 + j) % 4]
                            tcopy(POOL, y.t[:, 0:2], halo.t[:, fci, :], [halo], [y])
                            P.op(ACT, (lambda e, y=y, ps=ps: e.copy(out=y.t[:, 2:TT + 2], in_=ps.t[:, :])), [ps], [y])
                            tcopy(POOL, halo.t[:, fci, :], y.t[:, TT:TT + 2], [y], [halo])
                            fcol = lambda k, fci=fci: fw.t[:, (l * 3 + k) * 88 + fci:(l * 3 + k) * 88 + fci + 1]
                            tsc(DVE, z.t[:, :], y.t[:, 0:TT], fcol(0), None, ALU.mult, None, [y, fw], [z])
                            stt(DVE, z.t[:, :], y.t[:, 1:TT + 1], fcol(1), z.t[:, :], ALU.mult, ALU.add, [y, fw, z], [z])
                            stt(DVE, z.t[:, :], y.t[:, 2:TT + 2], fcol(2), z.t[:, :], ALU.mult, ALU.add, [y, fw, z], [z])
                            zz.append(z)
                        act(zz[0].t[:, :], zz[0].t[:, :], AF.Gelu_apprx_tanh, [zz[0]], [zz[0]])
                        tt(DVE, aT.t[:, g, :], zz[0].t[:, :], zz[1].t[:, :], ALU.mult, [zz[0], zz[1]], [aT])
                    pend = None
                    for dc in range(DC):
                        gtick()
                        ps = nextps()
                        for hf in range(2):
                            w = wdn[didx % 4]
                            dload()
                            didx += 1
                            for f in range(FC // 2):
                                ff = hf * (FC // 2) + f
                                mm(ps.t[:, :], w.t[:, f, :], aT.t[:, ff, :], ff == 0, ff == FC - 1, [w, aT], [ps])
                        if pend is not None:
                            mm(psr.t[:, :], ones_bf.t[:, :], pend[0].t[:, :], pend[1] == 0, pend[1] == DC - 1, [ones_bf, pend[0]], [psr])
                        if dc % 2 == 0:
                            P.op(ACT, (lambda e, ps=ps, dc=dc: e.copy(out=ys.t[:, dc, :], in_=ps.t[:, :])), [ps], [ysc[dc]])
                        else:
                            tcopy(DVE, ys.t[:, dc, :], ps.t[:, :], [ps], [ysc[dc]])
                        s = sq[dc % 2]
                        act(s.t[:, :], ys.t[:, dc, :], AF.Square, [ysc[dc]], [s])
                        pend = (s, dc)
                    mm(psr.t[:, :], ones_bf.t[:, :], pend[0].t[:, :], pend[1] == 0, pend[1] == DC - 1, [ones_bf, pend[0]], [psr])
                    act(rtmp.t[:, :], psr.t[:, :], AF.Sqrt, [psr], [rtmp], scale=1.0 / D, bias=EPS)
                    P.op(DVE, lambda e: e.reciprocal(out=r.t[:, :], in_=rtmp.t[:, :]), [rtmp], [r])
                    for c in range(DC):
                        stt(DVE, ys.t[:, c, :], ys.t[:, c, :], gcol(3, l, c), r.t[:, :], ALU.mult, ALU.mult, [ysc[c], gv, r], [ysc[c]])
                        tt(DVE, xT.t[:, c, :], xT.t[:, c, :], ys.t[:, c, :], ALU.add, [xT, ysc[c]], [xT])
                    dma(POOL, XTv[:, :, t0:t0 + TT], xT.t[:, :, :], xT, BXT)
                P.emit()

        prep_q.extend(prep_units(layers[0]))
        phase_S()
        prep_pull(21)
        phase_T()
        for li, l in enumerate(layers):
            prep_pull(10 ** 6)
            with ExitStack() as sab:
                lfh["lf"] = sbt(sab, "lf_all", [8, S], F32, True)
                if "A" in phases:
                    phase_A(l)
                if "B" in phases:
                    phase_B(l)
            if "C" in phases:
                phase_C(l)
            if li + 1 < len(layers):
                prep_q.extend(prep_units(layers[li + 1]))
            if "D" in phases:
                phase_D(l, li == len(layers) - 1)
        phase_U()
    return nc


_IN_NAMES = ["x", "pre_mix_g", "w_in", "b_forget", "conv_w", "conv_b", "conv_ln_g", "conv_ln_b", "w_out",
             "post_mix_g", "pre_ffn_g", "w_up", "ffn_conv_w", "w_down", "post_ffn_g"]


def run_layers(layers, inputs, x, dbg=False, phases="ABCD", ncores=8):
    layers = list(layers)
    nc = build_nc(list(range(len(layers))), dbg=dbg, phases=phases)
    shared = {k: np.ascontiguousarray(np.asarray(inputs[k], dtype=np.float32)[layers]) for k in _IN_NAMES if k != "x"}
    in_maps = []
    for b in range(ncores):
        m = dict(shared)
        m["x"] = np.ascontiguousarray(x[b])
        in_maps.append(m)
    res = run_bass_kernel_spmd(nc, in_maps, core_ids=list(range(ncores)))
    return res


def kernel(**inputs):
    x = np.asarray(inputs["x"], dtype=np.float32)
    res = run_layers(list(range(DEPTH)), inputs, x)
    return np.stack([np.asarray(r["out"], dtype=np.float32) for r in res.results], axis=0)
```

```python
import numpy as np
from contextlib import ExitStack
import concourse.bass as bass
import concourse.mybir as mybir
from concourse.bass_utils import run_bass_kernel_spmd

F32 = mybir.dt.float32
BF16 = mybir.dt.bfloat16
AF = mybir.ActivationFunctionType
ALU = mybir.AluOpType

D = 2048
S = 4096
DEPTH = 4
TT = 512
NT = S // TT
DC = D // 128
CCH = 1024
NH = 8
HD = 128
DFF = 5632
FC = DFF // 128
INC = 5128
CK = 31
EPS = 1e-6
GW = 256
NEG = -30000.0


class Actor:
    def __init__(self, name, kind):
        self.name = name
        self.kind = kind
        self.n = 0
        self.ops = []
        self.marked = set()
        self.waited = {}
        self.sem = None
        self.rank = {}
        self.nrank = 0
        self.last_compute = 0
        self.nobarrier = False


class Buf:
    def __init__(self, name, t, persistent=False, nobarrier=False):
        self.name = name
        self.t = t
        self.lw = None
        self.rd = {}
        self.lane = None
        self.persistent = persistent
        self.nobarrier = nobarrier


class Prog:
    def __init__(self, nc, es, nlanes):
        self.nc = nc
        self.PE = Actor("pe", "eng")
        self.ACT = Actor("act", "eng")
        self.DVE = Actor("dve", "eng")
        self.POOL = Actor("pool", "eng")
        self.SP = Actor("sp", "eng")
        self.engs = [self.PE, self.ACT, self.DVE, self.POOL, self.SP]
        for a in self.engs:
            a.sem = es.enter_context(nc.semaphore(a.name))
        self.free_lanes = []
        self.all_lanes = []
        for i in range(nlanes):
            ln = Actor("lane%d" % i, "lane")
            ln.sem = es.enter_context(nc.semaphore(ln.name))
            self.free_lanes.append(ln)
            self.all_lanes.append(ln)
        self.phase_lanes = []

    def _need(self, eng, deps):
        best = {}
        for (a, i) in deps:
            if a is eng and eng is self.PE:
                continue
            if i <= eng.waited.get(a, 0):
                continue
            if i > best.get(a, 0):
                best[a] = i
        waits = []
        for a, i in best.items():
            eng.waited[a] = i
            if a.kind == "eng":
                a.marked.add(i)
            waits.append((a, i))
        return waits

    def op(self, eng, fn, reads=(), writes=()):
        deps = []
        for b in reads:
            if b.lw is not None:
                deps.append(b.lw)
        for b in writes:
            if b.lw is not None:
                deps.append(b.lw)
            deps.extend(b.rd.items())
        waits = self._need(eng, deps)
        eng.n += 1
        idx = eng.n
        eng.last_compute = idx
        eng.ops.append((fn, waits, idx, None))
        for b in reads:
            b.rd[eng] = idx
        for b in writes:
            b.lw = (eng, idx)
            b.rd = {}

    def dma(self, q, fn, src, dst, chain=True):
        if dst.lane is None:
            dst.lane = self.free_lanes.pop()
            dst.lane.nobarrier = dst.nobarrier
            if not dst.persistent:
                self.phase_lanes.append(dst)
        lane = dst.lane
        deps = []
        if src.lw is not None:
            deps.append(src.lw)
        if dst.lw is not None and (chain or dst.lw[0] is not lane):
            deps.append(dst.lw)
        deps.extend(dst.rd.items())
        if chain and lane.n > 0:
            deps.append((lane, lane.n))
        waits = self._need(q, deps)
        lane.n += 1
        q.n += 1
        q.ops.append((fn, waits, q.n, lane))
        src.rd[lane] = lane.n
        dst.lw = (lane, lane.n)
        dst.rd = {}

    def barrier(self):
        deps = [(e, e.last_compute) for e in self.engs if e.last_compute > 0]
        deps += [(ln, ln.n) for ln in self.all_lanes if ln.n > 0 and not ln.nobarrier]
        for e in self.engs:
            waits = self._need(e, deps)
            e.n += 1
            e.ops.append((None, waits, e.n, None))

    def emit(self):
        self.barrier()
        nc = self.nc
        for a in self.engs:
            for i in sorted(a.marked):
                if i not in a.rank:
                    a.nrank += 1
                    a.rank[i] = a.nrank
        with nc.Block() as block:
            def val(a, i):
                return a.rank[i] if a.kind == "eng" else 16 * i

            def run(actor):
                def body(h):
                    for (fn, waits, idx, lane) in actor.ops:
                        for (a, i) in waits:
                            h.wait_ge(a.sem, val(a, i))
                        if fn is None:
                            continue
                        ins = fn(h)
                        if lane is not None:
                            ins.then_inc(lane.sem, 16)
                        elif idx in actor.marked:
                            ins.then_inc(actor.sem, 1)
                    actor.ops = []
                return body

            block.tensor(run(self.PE))
            block.scalar(run(self.ACT))
            block.vector(run(self.DVE))
            block.gpsimd(run(self.POOL))
            block.sync(run(self.SP))
        for a in self.engs:
            a.marked = set()
        for b in self.phase_lanes:
            b.lane.nobarrier = False
            self.free_lanes.append(b.lane)
            b.lane = None
        self.phase_lanes = []


def build_nc(layers, dbg=False, phases="ABCD"):
    nc = bass.Bass("TRN2", target_bir_lowering=False)
    L = len(layers)

    def din(name, shape):
        return nc.dram_tensor(name, shape, F32, kind="ExternalInput")

    x_in = din("x", [S, D])
    pre_mix_g = din("pre_mix_g", [L, D])
    w_in = din("w_in", [L, D, INC])
    b_forget = din("b_forget", [L, NH])
    conv_w = din("conv_w", [L, CK, CCH])
    conv_b = din("conv_b", [L, CCH])
    conv_ln_g = din("conv_ln_g", [L, CCH])
    conv_ln_b = din("conv_ln_b", [L, CCH])
    w_out = din("w_out", [L, D, D])
    post_mix_g = din("post_mix_g", [L, D])
    pre_ffn_g = din("pre_ffn_g", [L, D])
    w_up = din("w_up", [L, D, 2 * DFF])
    ffn_conv_w = din("ffn_conv_w", [L, 3, 2 * DFF])
    w_down = din("w_down", [L, DFF, D])
    post_ffn_g = din("post_ffn_g", [L, D])
    out = nc.dram_tensor("out", [S, D], F32, kind="ExternalOutput")

    skind = "ExternalOutput" if dbg else "Internal"
    XT = nc.dram_tensor("XT", [D, S], F32, kind=skind)
    QT = nc.dram_tensor("QT", [CCH, S], BF16, kind=skind)
    KT = nc.dram_tensor("KT", [CCH, S], BF16, kind=skind)
    VV = nc.dram_tensor("VV", [S, CCH], BF16, kind=skind)
    GT = nc.dram_tensor("GT", [CCH, S], BF16, kind=skind)
    MA = nc.dram_tensor("MA", [CCH, S], BF16, kind=skind)
    MB = nc.dram_tensor("MB", [CCH, S], BF16, kind=skind)
    CS = nc.dram_tensor("CS", [NH, 6, S], BF16, kind=skind)
    WINb = {}
    WFb = {}
    WOUTb = {}
    WUPb = {}
    WDNb = {}
    for l in layers:
        WINb[l] = nc.dram_tensor("WINb%d" % l, [20, 128, DC, GW], BF16)
        WFb[l] = nc.dram_tensor("WFb%d" % l, [128, DC, 8], BF16)
        WOUTb[l] = nc.dram_tensor("WOUTb%d" % l, [8, 128, DC, GW], BF16)
        WUPb[l] = nc.dram_tensor("WUPb%d" % l, [FC, 128, DC, GW], BF16)
        WDNb[l] = nc.dram_tensor("WDNb%d" % l, [DC, 128, FC, 128], BF16)

    with ExitStack() as es:
        P = Prog(nc, es, 92)
        PE, ACT, DVE, POOL, SP = P.PE, P.ACT, P.DVE, P.POOL, P.SP

        uid = [0]

        def sbt(scope, name, shape, dt, persistent=False):
            uid[0] += 1
            name = "%s_%d" % (name, uid[0])
            t = scope.enter_context(nc.sbuf_tensor(name, shape, dt))
            return Buf(name, t, persistent)

        def drb(name, t, nb=False):
            return Buf(name, t, True, nb)

        Bx = drb("x", x_in)
        Bout = drb("out", out)
        BXT = drb("XT", XT)
        BQT = drb("QT", QT)
        BKT = drb("KT", KT)
        BVV = drb("VV", VV)
        BGT = drb("GT", GT)
        BMA = drb("MA", MA)
        BMB = drb("MB", MB)
        BCS = drb("CS", CS)
        Bparam = drb("params", None)
        BWIN = {l: drb("WIN%d" % l, WINb[l], True) for l in layers}
        BWF = {l: drb("WF%d" % l, WFb[l], True) for l in layers}
        BWOUT = {l: drb("WOUT%d" % l, WOUTb[l], True) for l in layers}
        BWUP = {l: drb("WUP%d" % l, WUPb[l], True) for l in layers}
        BWDN = {l: drb("WDN%d" % l, WDNb[l], True) for l in layers}

        PS = []
        for i in range(8):
            t = es.enter_context(nc.psum_tensor("ps%d" % i, [128, 512], F32))
            PS.append(Buf("ps%d" % i, t, True))

        ones_bf = sbt(es, "ones_bf", [128, 128], BF16, True)
        ones_f = sbt(es, "ones_f", [128, 512], F32, True)
        ident_f = sbt(es, "ident_f", [128, 128], F32, True)
        ident_bf = sbt(es, "ident_bf", [128, 128], BF16, True)
        gv = sbt(es, "gv", [128, 4 * 64], F32, True)
        cv = sbt(es, "cv", [128, 3 * 32], F32, True)
        cw = sbt(es, "cw", [128, 1024], F32, True)
        fw = sbt(es, "fw", [128, 1152], F32, True)
        negb = sbt(es, "negb", [128, 4], F32, True)
        lfh = {}
        halo = sbt(es, "halo", [128, 2 * FC, 2], F32, True)

        def mm(ps_ap, lhsT, rhs, start, stop, reads, writes):
            P.op(PE, lambda e: e.matmul(ps_ap, lhsT=lhsT, rhs=rhs, start=start, stop=stop), reads, writes)

        def act(out_ap, in_ap, func, reads, writes, scale=1.0, bias=0.0):
            P.op(ACT, lambda e: e.activation(out=out_ap, in_=in_ap, func=func, bias=bias, scale=scale), reads, writes)

        def tcopy(eng, out_ap, in_ap, reads, writes):
            P.op(eng, lambda e: e.tensor_copy(out=out_ap, in_=in_ap), reads, writes)

        def tt(eng, out_ap, a, b, op, reads, writes):
            P.op(eng, lambda e: e.tensor_tensor(out=out_ap, in0=a, in1=b, op=op), reads, writes)

        def tsc(eng, out_ap, a, s1, s2, op0, op1, reads, writes):
            if op1 is None:
                P.op(eng, lambda e: e.tensor_scalar(out=out_ap, in0=a, scalar1=s1, scalar2=None, op0=op0), reads, writes)
            else:
                P.op(eng, lambda e: e.tensor_scalar(out=out_ap, in0=a, scalar1=s1, scalar2=s2, op0=op0, op1=op1), reads, writes)

        def stt(eng, out_ap, a, s, b, op0, op1, reads, writes):
            P.op(eng, lambda e: e.scalar_tensor_tensor(out=out_ap, in0=a, scalar=s, in1=b, op0=op0, op1=op1), reads, writes)

        def dma(q, out_ap, in_ap, src, dst, chain=True, **kw):
            P.dma(q, lambda e: e.dma_start(out=out_ap, in_=in_ap, **kw), src, dst, chain=chain)

        def prep_units(l):
            units = []
            for g in range(20):
                units.append((WINb[l][g], w_in[l, :, g * GW:(g + 1) * GW].rearrange("(c p) n -> p c n", p=128), BWIN[l]))
            units.append((WFb[l][:, :, :], w_in[l, :, 5120:5128].rearrange("(c p) n -> p c n", p=128), BWF[l]))
            for g in range(8):
                units.append((WOUTb[l][g], w_out[l, :, g * GW:(g + 1) * GW].rearrange("(c p) n -> p c n", p=128), BWOUT[l]))
            for g in range(FC):
                units.append((WUPb[l][g, :, :, 0:128], w_up[l, :, g * 128:(g + 1) * 128].rearrange("(c p) n -> p c n", p=128), BWUP[l]))
                units.append((WUPb[l][g, :, :, 128:256], w_up[l, :, DFF + g * 128:DFF + (g + 1) * 128].rearrange("(c p) n -> p c n", p=128), BWUP[l]))
            for g in range(DC):
                units.append((WDNb[l][g], w_down[l, :, g * 128:(g + 1) * 128].rearrange("(c p) n -> p c n", p=128), BWDN[l]))
            return units

        prep_q = []

        def prep_pull(n):
            for _ in range(n):
                if not prep_q:
                    return
                o, i, b = prep_q.pop(0)
                dma(POOL, o, i, Bparam, b, chain=True, max_dma_last_dim=2048)

        def phase_S():
            with ExitStack() as sc:
                ld = [sbt(sc, "ld%d" % i, [128, 128], F32) for i in range(2)]
                P.op(POOL, lambda e: e.memset(ones_f.t[:, :], 1.0), [], [ones_f])
                P.op(DVE, lambda e: e.memset(ones_bf.t[:, :], 1.0), [], [ones_bf])
                P.op(POOL, lambda e: e.memset(ident_f.t[:, :], 1.0), [], [ident_f])
                P.op(POOL, lambda e: e.affine_select(out=ident_f.t[:, :], in_=ident_f.t[:, :], pattern=[[-1, 128]],
                                                      compare_op=ALU.is_equal, fill=0.0, base=0, channel_multiplier=1),
                     [ident_f], [ident_f])
                tcopy(DVE, ident_bf.t[:, :], ident_f.t[:, :], [ident_f], [ident_bf])
                P.op(DVE, lambda e: e.memset(halo.t[:, :, :], 0.0), [], [halo])
                P.op(DVE, lambda e: e.memset(ld[0].t[:, :], 0.0), [], [ld[0]])
                P.op(DVE, lambda e: e.memset(ld[1].t[:, :], 0.0), [], [ld[1]])
                cnt = [0]

                def rows_T(src_ap, nrows, ncols, dst_buf, dst_ap, neg=False):
                    k = cnt[0] % 2
                    cnt[0] += 1
                    dma(SP, ld[k].t[0:nrows, 0:ncols], src_ap, Bparam, ld[k])
                    ps = PS[k]
                    P.op(PE, lambda e: e.transpose(ps.t[:, 0:128], ld[k].t[:, :], ident_f.t[:, :]), [ld[k], ident_f], [ps])
                    if neg:
                        tsc(DVE, dst_ap, ps.t[0:ncols, 0:nrows], -1.0, None, ALU.mult, None, [ps], [dst_buf])
                    else:
                        tcopy(DVE, dst_ap, ps.t[0:ncols, 0:nrows], [ps], [dst_buf])

                for kind, g in enumerate([pre_mix_g, post_mix_g, pre_ffn_g, post_ffn_g]):
                    rows_T(g.ap().rearrange("l (c p) -> (l c) p", p=128), 16 * L, 128, gv, gv.t[:, kind * 64:kind * 64 + 16 * L])
                for kind, g in enumerate([conv_b, conv_ln_g, conv_ln_b]):
                    rows_T(g.ap().rearrange("l (c p) -> (l c) p", p=128), 8 * L, 128, cv, cv.t[:, kind * 32:kind * 32 + 8 * L])
                cwv = conv_w.ap().rearrange("l k (c p) -> (l k c) p", p=128)
                for r0 in range(0, L * CK * 8, 128):
                    n = min(128, L * CK * 8 - r0)
                    rows_T(cwv[r0:r0 + n, :], n, 128, cw, cw.t[:, r0:r0 + n])
                fwv = ffn_conv_w.ap().rearrange("l k (c p) -> (l k c) p", p=128)
                for r0 in range(0, L * 3 * 88, 128):
                    n = min(128, L * 3 * 88 - r0)
                    rows_T(fwv[r0:r0 + n, :], n, 128, fw, fw.t[:, r0:r0 + n])
                rows_T(b_forget.ap(), L, 8, negb, negb.t[0:8, 0:L], neg=True)
                P.emit()

        def gcol(kind, l, c):
            return gv.t[:, kind * 64 + l * 16 + c:kind * 64 + l * 16 + c + 1]

        def phase_T():
            with ExitStack() as sc:
                xin = [sbt(sc, "xin%d" % i, [128, D], F32) for i in range(4)]
                st = [sbt(sc, "xst%d" % i, [128, DC, TT], F32) for i in range(2)]
                XTv = XT.ap().rearrange("(c p) t -> p c t", p=128)
                k = 0
                for ti in range(NT):
                    sb = st[ti % 2]
                    for s in range(4):
                        xb = xin[k % 4]
                        r0 = ti * TT + s * 128
                        dma(SP, xb.t[:, :], x_in[r0:r0 + 128, :], Bx, xb)
                        for c4 in range(4):
                            ps = PS[(k * 4 + c4) % 8]
                            for j in range(4):
                                c = c4 * 4 + j
                                P.op(PE, (lambda e, ps=ps, xb=xb, c=c, j=j: e.transpose(
                                    ps.t[:, j * 128:(j + 1) * 128], xb.t[:, c * 128:(c + 1) * 128], ident_f.t[:, :])),
                                     [xb, ident_f], [ps])
                            eng = ACT if c4 % 2 == 0 else DVE
                            o = sb.t[:, c4 * 4:(c4 + 1) * 4, s * 128:(s + 1) * 128]
                            i_ = ps.t[:, :].rearrange("p (j t) -> p j t", j=4)
                            if eng is ACT:
                                P.op(ACT, (lambda e, o=o, i_=i_: e.copy(out=o, in_=i_)), [ps], [sb])
                            else:
                                tcopy(DVE, o, i_, [ps], [sb])
                        k += 1
                    dma(ACT, XTv[:, :, ti * TT:(ti + 1) * TT], sb.t[:, :, :], sb, BXT)
                P.emit()

        def phase_U():
            with ExitStack() as sc:
                xt = [sbt(sc, "uxt%d" % i, [128, DC, TT], F32) for i in range(2)]
                os_ = [sbt(sc, "uos%d" % i, [128, D], F32) for i in range(2)]
                XTv = XT.ap().rearrange("(c p) t -> p c t", p=128)
                k = 0
                for ti in range(NT):
                    xb = xt[ti % 2]
                    dma(SP, xb.t[:, :, :], XTv[:, :, ti * TT:(ti + 1) * TT], BXT, xb)
                    for s in range(4):
                        ob = os_[k % 2]
                        for c4 in range(4):
                            ps = PS[(k * 4 + c4) % 8]
                            for j in range(4):
                                c = c4 * 4 + j
                                P.op(PE, (lambda e, ps=ps, xb=xb, c=c, j=j, s=s: e.transpose(
                                    ps.t[:, j * 128:(j + 1) * 128], xb.t[:, c, s * 128:(s + 1) * 128], ident_f.t[:, :])),
                                     [xb, ident_f], [ps])
                            o = ob.t[:, c4 * 512:(c4 + 1) * 512]
                            if c4 % 2 == 0:
                                P.op(ACT, (lambda e, o=o, ps=ps: e.copy(out=o, in_=ps.t[:, :])), [ps], [ob])
                            else:
                                tcopy(DVE, o, ps.t[:, :], [ps], [ob])
                        r0 = ti * TT + s * 128
                        dma(ACT, out[r0:r0 + 128, :], ob.t[:, :], ob, Bout)
                        k += 1
                P.emit()

        def rms_r(psb, sq, rtmp, r, src_buf, src_chunk, nchunks, n_feat):
            for c in range(nchunks):
                s = sq[c % len(sq)]
                act(s.t[:, :], src_chunk(c), AF.Square, [src_buf], [s])
                mm(psb.t[:, :], ones_bf.t[:, :], s.t[:, :], c == 0, c == nchunks - 1, [ones_bf, s], [psb])
            act(rtmp.t[:, :], psb.t[:, :], AF.Sqrt, [psb], [rtmp], scale=1.0 / n_feat, bias=EPS)
            P.op(DVE, lambda e: e.reciprocal(out=r.t[:, :], in_=rtmp.t[:, :]), [rtmp], [r])

        def phase_A(l):
            lf_all = lfh["lf"]
            with ExitStack() as sc:
                xTs = [sbt(sc, "a_xT%d" % i, [128, DC, TT], F32) for i in range(2)]
                hTs = [sbt(sc, "a_hT%d" % i, [128, DC, TT], BF16) for i in range(2)]
                wsl = [sbt(sc, "a_w%d" % i, [128, DC, GW], BF16) for i in range(3)]
                wf = sbt(sc, "a_wf", [128, DC, 8], BF16)
                sq = [sbt(sc, "a_sq%d" % i, [128, TT], BF16) for i in range(2)]
                rtmp = sbt(sc, "a_rtmp", [128, TT], F32)
                r = sbt(sc, "a_r", [128, TT], F32)
                a_st = sbt(sc, "a_ast", [128, 8, TT], BF16)
                g_st = sbt(sc, "a_gst", [128, 8, TT], BF16)
                q_st = sbt(sc, "a_qst", [128, 8, TT], BF16)
                k_st = sbt(sc, "a_kst", [128, 8, TT], BF16)
                v_st = sbt(sc, "a_vst", [128, 4, CCH], BF16)
                sig = [sbt(sc, "a_sig%d" % i, [128, TT], F32) for i in range(2)]
                e1 = sbt(sc, "a_e1", [8, TT], F32)
                XTv = XT.ap().rearrange("(c p) t -> p c t", p=128)
                dma(SP, wf.t[:, :, :], WFb[l][:, :, :], BWF[l], wf)
                seq = [(ti, g) for ti in range(NT) for g in range(20)]
                state = {"next": 0}

                def wload():
                    i = state["next"]
                    if i >= len(seq):
                        return
                    g = seq[i][1]
                    dma(SP, wsl[i % 3].t[:, :, :], WINb[l][g], BWIN[l], wsl[i % 3])
                    state["next"] += 1

                psr = PS[0]
                psf = PS[1]
                pm = PS[2:8]
                pmi = [0]

                def nextps():
                    p = pm[pmi[0] % 6]
                    pmi[0] += 1
                    return p

                widx = 0
                dma(SP, xTs[0].t[:, :, :], XTv[:, :, 0:TT], BXT, xTs[0])

                def norm(tj):
                    xT_ = xTs[tj % 2]
                    hT_ = hTs[tj % 2]
                    rms_r(psr, sq, rtmp, r, xT_, lambda c, xT_=xT_: xT_.t[:, c, :], DC, D)
                    for c in range(DC):
                        stt(DVE, hT_.t[:, c, :], xT_.t[:, c, :], gcol(0, l, c), r.t[:, :], ALU.mult, ALU.mult, [xT_, gv, r], [hT_])

                wload()
                wload()
                norm(0)
                for ti in range(NT):
                    t0 = ti * TT
                    hT = hTs[ti % 2]
                    if ti + 1 < NT:
                        dma(SP, xTs[(ti + 1) % 2].t[:, :, :], XTv[:, :, t0 + TT:t0 + 2 * TT], BXT, xTs[(ti + 1) % 2])
                    for g in range(20):
                        if g == 12 and ti + 1 < NT:
                            norm(ti + 1)
                        w = wsl[widx % 3]
                        wload()
                        widx += 1
                        prep_pull(1)
                        if g < 16:
                            for j in range(2):
                                ch = g * 2 + j
                                ps = nextps()
                                for c in range(DC):
                                    mm(ps.t[:, :], w.t[:, c, j * 128:(j + 1) * 128], hT.t[:, c, :], c == 0, c == DC - 1, [w, hT], [ps])
                                if ch < 8:
                                    tcopy(DVE, a_st.t[:, ch, :], ps.t[:, :], [ps], [a_st])
                                elif ch < 16:
                                    sg = sig[ch % 2]
                                    act(sg.t[:, :], ps.t[:, :], AF.Sigmoid, [ps], [sg])
                                    tt(DVE, g_st.t[:, ch - 8, :], a_st.t[:, ch - 8, :], sg.t[:, :], ALU.mult, [a_st, sg], [g_st])
                                elif ch < 24:
                                    act(q_st.t[:, ch - 16, :], ps.t[:, :], AF.Copy, [ps], [q_st], scale=float(HD) ** -0.5)
                                else:
                                    tcopy(DVE, k_st.t[:, ch - 24, :], ps.t[:, :], [ps], [k_st])
                        else:
                            for s in range(4):
                                ps = nextps()
                                for c in range(DC):
                                    mm(ps.t[:, 0:GW], hT.t[:, c, s * 128:(s + 1) * 128], w.t[:, c, :], c == 0, c == DC - 1, [w, hT], [ps])
                                o = v_st.t[:, s, (g - 16) * GW:(g - 15) * GW]
                                if s % 2 == 0:
                                    P.op(ACT, (lambda e, o=o, ps=ps: e.copy(out=o, in_=ps.t[:, 0:GW])), [ps], [v_st])
                                else:
                                    tcopy(DVE, o, ps.t[:, 0:GW], [ps], [v_st])
                        if g == 7:
                            dma(POOL, GT.ap().rearrange("(c p) t -> p c t", p=128)[:, :, t0:t0 + TT], g_st.t[:, :, :], g_st, BGT)
                        if g == 11:
                            dma(POOL, QT.ap().rearrange("(c p) t -> p c t", p=128)[:, :, t0:t0 + TT], q_st.t[:, :, :], q_st, BQT)
                        if g == 15:
                            dma(POOL, KT.ap().rearrange("(c p) t -> p c t", p=128)[:, :, t0:t0 + TT], k_st.t[:, :, :], k_st, BKT)
                    dma(POOL, VV.ap().rearrange("(s p) e -> p s e", p=128)[:, ti * 4:(ti + 1) * 4, :], v_st.t[:, :, :], v_st, BVV)
                    for c in range(DC):
                        mm(psf.t[0:8, :], wf.t[:, c, :], hT.t[:, c, :], c == 0, c == DC - 1, [wf, hT], [psf])
                    act(e1.t[:, :], psf.t[0:8, :], AF.Exp, [psf, negb], [e1], scale=-1.0, bias=negb.t[0:8, l:l + 1])
                    act(lf_all.t[:, t0:t0 + TT], e1.t[:, :], AF.Ln, [e1], [lf_all], scale=1.0, bias=1.0)
                P.emit()

        def phase_B(l):
            lf_all = lfh["lf"]
            with ExitStack() as sc:
                cs = sbt(sc, "b_cs", [8, S], F32)
                r1 = sbt(sc, "b_r1", [8, S], F32)
                pcs = sbt(sc, "b_pcs", [8, 6, S], BF16)
                for i in range(NT):
                    init = 0.0 if i == 0 else cs.t[:, i * TT - 1:i * TT]
                    P.op(DVE, (lambda e, i=i, init=init: e.tensor_tensor_scan(
                        out=cs.t[:, i * TT:(i + 1) * TT], data0=ones_f.t[0:8, :], data1=lf_all.t[:, i * TT:(i + 1) * TT],
                        initial=init, op0=ALU.mult, op1=ALU.add)), [ones_f, lf_all, cs], [cs])
                tsc(DVE, pcs.t[:, 0, :], cs.t[:, :], -1.0, None, ALU.mult, None, [cs], [pcs])
                stt(DVE, r1.t[:, :], cs.t[:, :], -1.0, pcs.t[:, 0, :], ALU.mult, ALU.subtract, [cs, pcs], [r1])
                tcopy(DVE, pcs.t[:, 1, :], r1.t[:, :], [r1], [pcs])
                tt(DVE, cs.t[:, :], r1.t[:, :], pcs.t[:, 1, :], ALU.subtract, [r1, pcs], [cs])
                tcopy(DVE, pcs.t[:, 2, :], cs.t[:, :], [cs], [pcs])
                for j in range(3):
                    tsc(DVE, pcs.t[:, 3 + j, :], pcs.t[:, j, :], -1.0, None, ALU.mult, None, [pcs], [pcs])
                dma(SP, CS.ap(), pcs.t[:, :, :], pcs, BCS)
                P.emit()

        def phase_C(l):
            with ExitStack() as sc:
                qT = [sbt(sc, "c_q%d" % i, [128, S], BF16) for i in range(2)]
                kT = [sbt(sc, "c_k%d" % i, [128, S], BF16) for i in range(2)]
                vv = [sbt(sc, "c_v%d" % i, [128, 32, HD], BF16) for i in range(2)]
                bl = [sbt(sc, "c_bl%d" % i, [128, S], BF16) for i in range(2)]
                br = [sbt(sc, "c_br%d" % i, [128, S], BF16) for i in range(2)]
                pT = [sbt(sc, "c_p%d" % i, [128, TT], BF16) for i in range(3)]
                ost = [sbt(sc, "c_o%d" % i, [128, TT], BF16) for i in range(3)]
                rl = [sbt(sc, "c_rl%d" % i, [128, TT], F32) for i in range(2)]
                gl = [sbt(sc, "c_gl%d" % i, [128, 8, TT + 32], BF16) for i in range(2)]
                acc = sbt(sc, "c_acc", [128, 8, TT], F32)
                csq = sbt(sc, "c_sq", [128, TT], F32)
                mean = sbt(sc, "c_mean", [128, TT], F32)
                msq = sbt(sc, "c_msq", [128, TT], F32)
                var = sbt(sc, "c_var", [128, TT], F32)
                rstd = sbt(sc, "c_rstd", [128, TT], F32)
                tmp = [sbt(sc, "c_tmp%d" % i, [128, TT], F32) for i in range(2)]
                a_st = sbt(sc, "c_ast", [128, 8, TT], BF16)
                maskb = sbt(sc, "c_maskb", [128, 4, 512], BF16)
                mk = sbt(sc, "c_mk", [128, 4, 512], F32)
                P.op(POOL, lambda e: e.memset(mk.t[:, :, :], 0.0), [], [mk])
                for i in range(4):
                    P.op(POOL, (lambda e, i=i: e.affine_select(out=mk.t[:, i, :], in_=mk.t[:, i, :], pattern=[[1, 512]],
                                                                compare_op=ALU.is_ge, fill=NEG, base=-128 * i,
                                                                channel_multiplier=-1)), [mk], [mk])
                tcopy(DVE, maskb.t[:, :, :], mk.t[:, :, :], [mk], [maskb])
                for i in range(2):
                    P.op(DVE, (lambda e, i=i: e.memset(bl[i].t[:, :], 0.0)), [], [bl[i]])
                    P.op(DVE, (lambda e, i=i: e.memset(br[i].t[:, :], 0.0)), [], [br[i]])
                    P.op(DVE, (lambda e, i=i: e.memset(bl[i].t[0:6, :], 1.0)), [], [bl[i]])
                    P.op(DVE, (lambda e, i=i: e.memset(br[i].t[0:6, :], 1.0)), [], [br[i]])
                    P.op(DVE, (lambda e, i=i: e.memset(gl[i].t[:, :, :], 0.0)), [], [gl[i]])
                pst = PS[6]
                pcv = PS[6]
                dg = [sbt(sc, "c_dg%d" % i, [128, CK, 128], BF16) for i in range(2)]
                GTv = GT.ap().rearrange("(c p) t -> p c t", p=128)
                MAv = MA.ap().rearrange("(c p) t -> p c t", p=128)

                def conv_gen():
                    HAL = 32
                    cwl = cw.t[:, l * CK * 8:(l + 1) * CK * 8].rearrange("p (k c) -> p k c", c=8)

                    def build(n):
                        cc_ = n % 8
                        dgt_ = dg[n % 2]
                        tt(DVE, dgt_.t[:, :, :], ident_bf.t[:, :].unsqueeze(1).broadcast_to([128, CK, 128]),
                           cwl[:, :, cc_].unsqueeze(2).broadcast_to([128, CK, 128]), ALU.mult, [ident_bf, cw], [dgt_])

                    build(0)
                    for ti in range(NT):
                        g = gl[ti % 2]
                        t0 = ti * TT
                        if ti == 0:
                            dma(SP, g.t[:, :, HAL:HAL + TT], GTv[:, :, 0:TT], BGT, g)
                        else:
                            dma(SP, g.t[:, :, 0:HAL + TT], GTv[:, :, t0 - HAL:t0 + TT], BGT, g)
                        yield
                        for cc in range(8):
                            n_ = ti * 8 + cc
                            dgt = dg[n_ % 2]
                            if n_ + 1 < NT * 8:
                                build(n_ + 1)
                            for k in range(CK):
                                mm(pcv.t[:, :], dgt.t[:, k, :], g.t[:, cc, 2 + k:2 + k + TT], k == 0, k == CK - 1, [dgt, g], [pcv])
                            act(acc.t[:, cc, :], pcv.t[:, :], AF.Identity, [pcv, cv], [acc], scale=1.0,
                                bias=cv.t[:, l * 8 + cc:l * 8 + cc + 1])
                            yield
                        for cc in range(8):
                            mm(pst.t[:, :], ones_f.t[:, 0:128], acc.t[:, cc, :], cc == 0, cc == 7, [ones_f, acc], [pst])
                        act(mean.t[:, :], pst.t[:, :], AF.Copy, [pst], [mean], scale=1.0 / CCH)
                        act(msq.t[:, :], pst.t[:, :], AF.Square, [pst], [msq], scale=1.0 / CCH)
                        for cc in range(8):
                            act(csq.t[:, :], acc.t[:, cc, :], AF.Square, [acc], [csq])
                            mm(pst.t[:, :], ones_f.t[:, 0:128], csq.t[:, :], cc == 0, cc == 7, [ones_f, csq], [pst])
                        yield
                        stt(DVE, var.t[:, :], pst.t[:, :], 1.0 / CCH, msq.t[:, :], ALU.mult, ALU.subtract, [pst, msq], [var])
                        act(var.t[:, :], var.t[:, :], AF.Sqrt, [var], [var], scale=1.0, bias=EPS)
                        P.op(DVE, lambda e: e.reciprocal(out=rstd.t[:, :], in_=var.t[:, :]), [var], [rstd])
                        for cc in range(8):
                            tm = tmp[cc % 2]
                            tt(DVE, tm.t[:, :], acc.t[:, cc, :], mean.t[:, :], ALU.subtract, [acc, mean], [tm])
                            tt(DVE, tm.t[:, :], tm.t[:, :], rstd.t[:, :], ALU.mult, [tm, rstd], [tm])
                            act(a_st.t[:, cc, :], tm.t[:, :], AF.Silu, [tm, cv], [a_st],
                                scale=cv.t[:, 32 + l * 8 + cc:32 + l * 8 + cc + 1], bias=cv.t[:, 64 + l * 8 + cc:64 + l * 8 + cc + 1])
                        dma(POOL, MAv[:, :, t0:t0 + TT], a_st.t[:, :, :], a_st, BMA)
                        yield
                    while True:
                        yield

                cg = conv_gen()
                n_units = NT * (1 + 8 + 2)
                pulled = [0]

                def conv_pull(target):
                    while pulled[0] < target:
                        next(cg)
                        pulled[0] += 1

                def head_load(h):
                    sl = h % 2
                    dma(SP, qT[sl].t[:, :], QT[h * 128:(h + 1) * 128, :], BQT, qT[sl])
                    dma(SP, kT[sl].t[:, :], KT[h * 128:(h + 1) * 128, :], BKT, kT[sl])
                    dma(SP, vv[sl].t[:, :, :], VV.ap().rearrange("(b p) e -> p b e", p=128)[:, :, h * HD:(h + 1) * HD], BVV, vv[sl])
                    dma(SP, br[sl].t[0:3, :], CS[h, 0:3, :], BCS, br[sl])
                    dma(SP, bl[sl].t[3:6, :], CS[h, 3:6, :], BCS, bl[sl])

                head_load(0)
                qtc = 0
                total_blocks = NH * sum(4 * j + 4 for j in range(NT))
                done_blocks = 0
                for h in range(NH):
                    sl = h % 2
                    if h + 1 < NH:
                        head_load(h + 1)
                    for j in range(NT):
                        q0 = j * TT
                        nb = 4 * j + 4
                        po = PS[3 + (qtc % 2)]
                        pl = PS[5] if qtc % 2 == 0 else PS[7]

                        def QK(i):
                            ps = PS[i % 3]
                            diag = i >= 4 * j
                            mm(ps.t[:, :], kT[sl].t[:, i * 128:(i + 1) * 128], qT[sl].t[:, q0:q0 + TT], True, False, [kT[sl], qT[sl]], [ps])
                            mm(ps.t[:, :], bl[sl].t[:, i * 128:(i + 1) * 128], br[sl].t[:, q0:q0 + TT], False, not diag, [bl[sl], br[sl]], [ps])
                            if diag:
                                mm(ps.t[:, :], ident_bf.t[:, :], maskb.t[:, i - 4 * j, :], False, True, [ident_bf, maskb], [ps])

                        def PV(i):
                            p = pT[i % 3]
                            act(p.t[:, :], PS[i % 3].t[:, :], AF.Exp, [PS[i % 3]], [p])
                            mm(po.t[:, :], vv[sl].t[:, i, :], p.t[:, :], i == 0, i == nb - 1, [vv[sl], p], [po])
                            mm(pl.t[:, :], ones_bf.t[:, :], p.t[:, :], i == 0, i == nb - 1, [ones_bf, p], [pl])

                        QK(0)
                        if nb > 1:
                            QK(1)
                        for i in range(nb):
                            if i + 2 < nb:
                                QK(i + 2)
                            PV(i)
                        rr = rl[qtc % 2]
                        oo = ost[qtc % 3]
                        P.op(DVE, (lambda e, rr=rr, pl=pl: e.reciprocal(out=rr.t[:, :], in_=pl.t[:, :])), [pl], [rr])
                        tt(DVE, oo.t[:, :], po.t[:, :], rr.t[:, :], ALU.mult, [po, rr], [oo])
                        dma(POOL, MB[h * 128:(h + 1) * 128, q0:q0 + TT], oo.t[:, :], oo, BMB)
                        qtc += 1
                        done_blocks += nb
                        conv_pull(int(n_units * done_blocks / total_blocks) + 1)
                conv_pull(n_units + 4)
                P.emit()

        def phase_D(l, last):
            with ExitStack() as sc:
                xT = sbt(sc, "d_xT", [128, DC, TT], F32)
                hT = sbt(sc, "d_hT", [128, DC, TT], BF16)
                aT = sbt(sc, "d_aT", [128, FC, TT], BF16)
                ys = sbt(sc, "d_ys", [128, DC, TT], F32)
                ysc = [Buf("ysc%d" % i, ys.t) for i in range(DC)]
                wsl = [sbt(sc, "d_w%d" % i, [128, DC, GW], BF16) for i in range(3)]
                wdn = [sbt(sc, "d_wd%d" % i, [128, FC // 2, 128], BF16) for i in range(4)]
                sq = [sbt(sc, "d_sq%d" % i, [128, TT], BF16) for i in range(2)]
                rtmp = sbt(sc, "d_rtmp", [128, TT], F32)
                r = sbt(sc, "d_r", [128, TT], F32)
                yb = [sbt(sc, "d_yb%d" % i, [128, TT + 2], F32) for i in range(2)]
                zc = [sbt(sc, "d_zc%d" % i, [128, TT], F32) for i in range(4)]
                P.op(DVE, lambda e: e.memset(halo.t[:, :, :], 0.0), [], [halo])
                XTv = XT.ap().rearrange("(c p) t -> p c t", p=128)
                MAv = MA.ap().rearrange("(c p) t -> p c t", p=128)
                MBv = MB.ap().rearrange("(c p) t -> p c t", p=128)
                seq = []
                for ti in range(NT):
                    seq += [("o", g) for g in range(8)] + [("u", g) for g in range(FC)]
                state = {"next": 0, "dnext": 0}

                def wload():
                    i = state["next"]
                    if i >= len(seq):
                        return
                    kind, g = seq[i]
                    if kind == "o":
                        dma(SP, wsl[i % 3].t[:, :, :], WOUTb[l][g], BWOUT[l], wsl[i % 3])
                    else:
                        dma(SP, wsl[i % 3].t[:, :, :], WUPb[l][g], BWUP[l], wsl[i % 3])
                    state["next"] += 1

                def dload():
                    i = state["dnext"]
                    if i >= NT * DC * 2:
                        return
                    dc_, hf = (i // 2) % DC, i % 2
                    dma(SP, wdn[i % 4].t[:, :, :], WDNb[l][dc_][:, hf * 22:(hf + 1) * 22, :], BWDN[l], wdn[i % 4])
                    state["dnext"] += 1

                gcnt = [0]

                def gtick():
                    gcnt[0] += 1
                    if gcnt[0] % 4 == 0:
                        prep_pull(1)

                psr = PS[0]
                pm = PS[1:8]
                pmi = [0]

                def nextps():
                    p = pm[pmi[0] % 7]
                    pmi[0] += 1
                    return p

                widx = 0
                didx = 0
                wload()
                wload()
                dload()
                dload()
                dload()
                for ti in range(NT):
                    t0 = ti * TT
                    dma(SP, hT.t[:, 0:8, :], MAv[:, :, t0:t0 + TT], BMA, hT)
                    dma(SP, hT.t[:, 8:16, :], MBv[:, :, t0:t0 + TT], BMB, hT)
                    chunk = 0
                    pend = None
                    for g in range(8):
                        w = wsl[widx % 3]
                        wload()
                        widx += 1
                        gtick()
                        for j in range(2):
                            ps = nextps()
                            for c in range(DC):
                                mm(ps.t[:, :], w.t[:, c, j * 128:(j + 1) * 128], hT.t[:, c, :], c == 0, c == DC - 1, [w, hT], [ps])
                            if pend is not None:
                                mm(psr.t[:, :], ones_bf.t[:, :], pend[0].t[:, :], pend[1] == 0, pend[1] == DC - 1, [ones_bf, pend[0]], [psr])
                            if chunk % 2 == 0:
                                P.op(ACT, (lambda e, ps=ps, chunk=chunk: e.copy(out=ys.t[:, chunk, :], in_=ps.t[:, :])), [ps], [ysc[chunk]])
                            else:
                                tcopy(DVE, ys.t[:, chunk, :], ps.t[:, :], [ps], [ysc[chunk]])
                            s = sq[chunk % 2]
                            act(s.t[:, :], ys.t[:, chunk, :], AF.Square, [ysc[chunk]], [s])
                            pend = (s, chunk)
                            chunk += 1
                    mm(psr.t[:, :], ones_bf.t[:, :], pend[0].t[:, :], pend[1] == 0, pend[1] == DC - 1, [ones_bf, pend[0]], [psr])
                    dma(SP, xT.t[:, :, :], XTv[:, :, t0:t0 + TT], BXT, xT)
                    act(rtmp.t[:, :], psr.t[:, :], AF.Sqrt, [psr], [rtmp], scale=1.0 / D, bias=EPS)
                    P.op(DVE, lambda e: e.reciprocal(out=r.t[:, :], in_=rtmp.t[:, :]), [rtmp], [r])
                    for c in range(DC):
                        stt(DVE, ys.t[:, c, :], ys.t[:, c, :], gcol(1, l, c), r.t[:, :], ALU.mult, ALU.mult, [ysc[c], gv, r], [ysc[c]])
                        tt(DVE, xT.t[:, c, :], xT.t[:, c, :], ys.t[:, c, :], ALU.add, [xT, ysc[c]], [xT])
                        s = sq[c % 2]
                        act(s.t[:, :], xT.t[:, c, :], AF.Square, [xT], [s])
                        mm(psr.t[:, :], ones_bf.t[:, :], s.t[:, :], c == 0, c == DC - 1, [ones_bf, s], [psr])
                    act(rtmp.t[:, :], psr.t[:, :], AF.Sqrt, [psr], [rtmp], scale=1.0 / D, bias=EPS)
                    P.op(DVE, lambda e: e.reciprocal(out=r.t[:, :], in_=rtmp.t[:, :]), [rtmp], [r])
                    for c in range(DC):
                        stt(DVE, hT.t[:, c, :], xT.t[:, c, :], gcol(2, l, c), r.t[:, :], ALU.mult, ALU.mult, [xT, gv, r], [hT])
                    for g in range(FC):
                        w = wsl[widx % 3]
                        wload()
                        widx += 1
                        gtick()
                        zz = []
                        for j in range(2):
                            fci = g + j * FC
                            ps = nextps()
                            for c in range(DC):
                                mm(ps.t[:, :], w.t[:, c, j * 128:(j + 1) * 128], hT.t[:, c, :], c == 0, c == DC - 1, [w, hT], [ps])
                            y = yb[(g * 2 + j) % 2]
                            z = zc[(g * 2 + j) % 4]
                            tcopy(POOL, y.t[:, 0:2], halo.t[:, fci, :], [halo], [y])
                            P.op(ACT, (lambda e, y=y, ps=ps: e.copy(out=y.t[:, 2:TT + 2], in_=ps.t[:, :])), [ps], [y])
                            tcopy(POOL, halo.t[:, fci, :], y.t[:, TT:TT + 2], [y], [halo])
                            fcol = lambda k, fci=fci: fw.t[:, (l * 3 + k) * 88 + fci:(l * 3 + k) * 88 + fci + 1]
                            tsc(DVE, z.t[:, :], y.t[:, 0:TT], fcol(0), None, ALU.mult, None, [y, fw], [z])
                            stt(DVE, z.t[:, :], y.t[:, 1:TT + 1], fcol(1), z.t[:, :], ALU.mult, ALU.add, [y, fw, z], [z])
                            stt(DVE, z.t[:, :], y.t[:, 2:TT + 2], fcol(2), z.t[:, :], ALU.mult, ALU.add, [y, fw, z], [z])
                            zz.append(z)
                        act(zz[0].t[:, :], zz[0].t[:, :], AF.Gelu_apprx_tanh, [zz[0]], [zz[0]])
                        tt(DVE, aT.t[:, g, :], zz[0].t[:, :], zz[1].t[:, :], ALU.mult, [zz[0], zz[1]], [aT])
                    pend = None
                    for dc in range(DC):
                        gtick()
                        ps = nextps()
                        for hf in range(2):
                            w = wdn[didx % 4]
                            dload()
                            didx += 1
                            for f in range(FC // 2):
                                ff = hf * (FC // 2) + f
                                mm(ps.t[:, :], w.t[:, f, :], aT.t[:, ff, :], ff == 0, ff == FC - 1, [w, aT], [ps])
                        if pend is not None:
                            mm(psr.t[:, :], ones_bf.t[:, :], pend[0].t[:, :], pend[1] == 0, pend[1] == DC - 1, [ones_bf, pend[0]], [psr])
                        if dc % 2 == 0:
                            P.op(ACT, (lambda e, ps=ps, dc=dc: e.copy(out=ys.t[:, dc, :], in_=ps.t[:, :])), [ps], [ysc[dc]])
                        else:
                            tcopy(DVE, ys.t[:, dc, :], ps.t[:, :], [ps], [ysc[dc]])
                        s = sq[dc % 2]
                        act(s.t[:, :], ys.t[:, dc, :], AF.Square, [ysc[dc]], [s])
                        pend = (s, dc)
                    mm(psr.t[:, :], ones_bf.t[:, :], pend[0].t[:, :], pend[1] == 0, pend[1] == DC - 1, [ones_bf, pend[0]], [psr])
                    act(rtmp.t[:, :], psr.t[:, :], AF.Sqrt, [psr], [rtmp], scale=1.0 / D, bias=EPS)
                    P.op(DVE, lambda e: e.reciprocal(out=r.t[:, :], in_=rtmp.t[:, :]), [rtmp], [r])
                    for c in range(DC):
                        stt(DVE, ys.t[:, c, :], ys.t[:, c, :], gcol(3, l, c), r.t[:, :], ALU.mult, ALU.mult, [ysc[c], gv, r], [ysc[c]])
                        tt(DVE, xT.t[:, c, :], xT.t[:, c, :], ys.t[:, c, :], ALU.add, [xT, ysc[c]], [xT])
                    dma(POOL, XTv[:, :, t0:t0 + TT], xT.t[:, :, :], xT, BXT)
                P.emit()

        prep_q.extend(prep_units(layers[0]))
        phase_S()
        prep_pull(21)
        phase_T()
        for li, l in enumerate(layers):
            if li > 0:
                prep_pull(10 ** 6)
            with ExitStack() as sab:
                lfh["lf"] = sbt(sab, "lf_all", [8, S], F32, True)
                if "A" in phases:
                    phase_A(l)
                if "B" in phases:
                    phase_B(l)
            prep_pull(10 ** 6)
            if "C" in phases:
                phase_C(l)
            if li + 1 < len(layers):
                prep_q.extend(prep_units(layers[li + 1]))
            if "D" in phases:
                phase_D(l, li == len(layers) - 1)
        phase_U()
    return nc


_IN_NAMES = ["x", "pre_mix_g", "w_in", "b_forget", "conv_w", "conv_b", "conv_ln_g", "conv_ln_b", "w_out",
             "post_mix_g", "pre_ffn_g", "w_up", "ffn_conv_w", "w_down", "post_ffn_g"]


def run_layers(layers, inputs, x, dbg=False, phases="ABCD", ncores=8):
    layers = list(layers)
    nc = build_nc(list(range(len(layers))), dbg=dbg, phases=phases)
    shared = {k: np.ascontiguousarray(np.asarray(inputs[k], dtype=np.float32)[layers]) for k in _IN_NAMES if k != "x"}
    in_maps = []
    for b in range(ncores):
        m = dict(shared)
        m["x"] = np.ascontiguousarray(x[b])
        in_maps.append(m)
    res = run_bass_kernel_spmd(nc, in_maps, core_ids=list(range(ncores)))
    return res


def kernel(**inputs):
    x = np.asarray(inputs["x"], dtype=np.float32)
    res = run_layers(list(range(DEPTH)), inputs, x)
    return np.stack([np.asarray(r["out"], dtype=np.float32) for r in res.results], axis=0)
```

```python
import numpy as np
from contextlib import ExitStack
import concourse.bass as bass
import concourse.mybir as mybir
from concourse.bass_utils import run_bass_kernel_spmd

F32 = mybir.dt.float32
BF16 = mybir.dt.bfloat16
AF = mybir.ActivationFunctionType
ALU = mybir.AluOpType

D = 2048
S = 4096
DEPTH = 4
TT = 512
NT = S // TT
DC = D // 128
CCH = 1024
NH = 8
HD = 128
DFF = 5632
FC = DFF // 128
INC = 5128
CK = 31
EPS = 1e-6
GW = 256
NEG = -30000.0


class Actor:
    def __init__(self, name, kind):
        self.name = name
        self.kind = kind
        self.n = 0
        self.ops = []
        self.marked = set()
        self.waited = {}
        self.sem = None
        self.rank = {}
        self.nrank = 0
        self.last_compute = 0
        self.nobarrier = False


class Buf:
    def __init__(self, name, t, persistent=False, nobarrier=False):
        self.name = name
        self.t = t
        self.lw = None
        self.rd = {}
        self.lane = None
        self.persistent = persistent
        self.nobarrier = nobarrier


class Prog:
    def __init__(self, nc, es, nlanes):
        self.nc = nc
        self.PE = Actor("pe", "eng")
        self.ACT = Actor("act", "eng")
        self.DVE = Actor("dve", "eng")
        self.POOL = Actor("pool", "eng")
        self.SP = Actor("sp", "eng")
        self.engs = [self.PE, self.ACT, self.DVE, self.POOL, self.SP]
        for a in self.engs:
            a.sem = es.enter_context(nc.semaphore(a.name))
        self.free_lanes = []
        self.all_lanes = []
        for i in range(nlanes):
            ln = Actor("lane%d" % i, "lane")
            ln.sem = es.enter_context(nc.semaphore(ln.name))
            self.free_lanes.append(ln)
            self.all_lanes.append(ln)
        self.phase_lanes = []

    def _need(self, eng, deps):
        best = {}
        for (a, i) in deps:
            if a is eng and eng is self.PE:
                continue
            if i <= eng.waited.get(a, 0):
                continue
            if i > best.get(a, 0):
                best[a] = i
        waits = []
        for a, i in best.items():
            eng.waited[a] = i
            if a.kind == "eng":
                a.marked.add(i)
            waits.append((a, i))
        return waits

    def op(self, eng, fn, reads=(), writes=()):
        deps = []
        for b in reads:
            if b.lw is not None:
                deps.append(b.lw)
        for b in writes:
            if b.lw is not None:
                deps.append(b.lw)
            deps.extend(b.rd.items())
        waits = self._need(eng, deps)
        eng.n += 1
        idx = eng.n
        eng.last_compute = idx
        eng.ops.append((fn, waits, idx, None))
        for b in reads:
            b.rd[eng] = idx
        for b in writes:
            b.lw = (eng, idx)
            b.rd = {}

    def dma(self, q, fn, src, dst, chain=True):
        if dst.lane is None:
            dst.lane = self.free_lanes.pop()
            dst.lane.nobarrier = dst.nobarrier
            if not dst.persistent:
                self.phase_lanes.append(dst)
        lane = dst.lane
        deps = []
        if src.lw is not None:
            deps.append(src.lw)
        if dst.lw is not None and (chain or dst.lw[0] is not lane):
            deps.append(dst.lw)
        deps.extend(dst.rd.items())
        if chain and lane.n > 0:
            deps.append((lane, lane.n))
        waits = self._need(q, deps)
        lane.n += 1
        q.n += 1
        q.ops.append((fn, waits, q.n, lane))
        src.rd[lane] = lane.n
        dst.lw = (lane, lane.n)
        dst.rd = {}

    def barrier(self):
        deps = [(e, e.last_compute) for e in self.engs if e.last_compute > 0]
        deps += [(ln, ln.n) for ln in self.all_lanes if ln.n > 0 and not ln.nobarrier]
        for e in self.engs:
            waits = self._need(e, deps)
            e.n += 1
            e.ops.append((None, waits, e.n, None))

    def emit(self):
        self.barrier()
        nc = self.nc
        for a in self.engs:
            for i in sorted(a.marked):
                if i not in a.rank:
                    a.nrank += 1
                    a.rank[i] = a.nrank
        with nc.Block() as block:
            def val(a, i):
                return a.rank[i] if a.kind == "eng" else 16 * i

            def run(actor):
                def body(h):
                    for (fn, waits, idx, lane) in actor.ops:
                        for (a, i) in waits:
                            h.wait_ge(a.sem, val(a, i))
                        if fn is None:
                            continue
                        ins = fn(h)
                        if lane is not None:
                            ins.then_inc(lane.sem, 16)
                        elif idx in actor.marked:
                            ins.then_inc(actor.sem, 1)
                    actor.ops = []
                return body

            block.tensor(run(self.PE))
            block.scalar(run(self.ACT))
            block.vector(run(self.DVE))
            block.gpsimd(run(self.POOL))
            block.sync(run(self.SP))
        for a in self.engs:
            a.marked = set()
        for b in self.phase_lanes:
            b.lane.nobarrier = False
            self.free_lanes.append(b.lane)
            b.lane = None
        self.phase_lanes = []


def build_nc(layers, dbg=False, phases="ABCD"):
    nc = bass.Bass("TRN2", target_bir_lowering=False)
    L = len(layers)

    def din(name, shape):
        return nc.dram_tensor(name, shape, F32, kind="ExternalInput")

    x_in = din("x", [S, D])
    pre_mix_g = din("pre_mix_g", [L, D])
    w_in = din("w_in", [L, D, INC])
    b_forget = din("b_forget", [L, NH])
    conv_w = din("conv_w", [L, CK, CCH])
    conv_b = din("conv_b", [L, CCH])
    conv_ln_g = din("conv_ln_g", [L, CCH])
    conv_ln_b = din("conv_ln_b", [L, CCH])
    w_out = din("w_out", [L, D, D])
    post_mix_g = din("post_mix_g", [L, D])
    pre_ffn_g = din("pre_ffn_g", [L, D])
    w_up = din("w_up", [L, D, 2 * DFF])
    ffn_conv_w = din("ffn_conv_w", [L, 3, 2 * DFF])
    w_down = din("w_down", [L, DFF, D])
    post_ffn_g = din("post_ffn_g", [L, D])
    out = nc.dram_tensor("out", [S, D], F32, kind="ExternalOutput")

    skind = "ExternalOutput" if dbg else "Internal"
    XT = nc.dram_tensor("XT", [D, S], F32, kind=skind)
    QT = nc.dram_tensor("QT", [CCH, S], BF16, kind=skind)
    KT = nc.dram_tensor("KT", [CCH, S], BF16, kind=skind)
    VV = nc.dram_tensor("VV", [S, CCH], BF16, kind=skind)
    GT = nc.dram_tensor("GT", [CCH, S], BF16, kind=skind)
    MA = nc.dram_tensor("MA", [CCH, S], BF16, kind=skind)
    MB = nc.dram_tensor("MB", [CCH, S], BF16, kind=skind)
    CS = nc.dram_tensor("CS", [NH, 6, S], BF16, kind=skind)
    WINb = {}
    WFb = {}
    WOUTb = {}
    WUPb = {}
    WDNb = {}
    for l in layers:
        WINb[l] = nc.dram_tensor("WINb%d" % l, [20, 128, DC, GW], BF16)
        WFb[l] = nc.dram_tensor("WFb%d" % l, [128, DC, 8], BF16)
        WOUTb[l] = nc.dram_tensor("WOUTb%d" % l, [8, 128, DC, GW], BF16)
        WUPb[l] = nc.dram_tensor("WUPb%d" % l, [FC, 128, DC, GW], BF16)
        WDNb[l] = nc.dram_tensor("WDNb%d" % l, [DC, 128, FC, 128], BF16)

    with ExitStack() as es:
        P = Prog(nc, es, 92)
        PE, ACT, DVE, POOL, SP = P.PE, P.ACT, P.DVE, P.POOL, P.SP

        uid = [0]

        def sbt(scope, name, shape, dt, persistent=False):
            uid[0] += 1
            name = "%s_%d" % (name, uid[0])
            t = scope.enter_context(nc.sbuf_tensor(name, shape, dt))
            return Buf(name, t, persistent)

        def drb(name, t, nb=False):
            return Buf(name, t, True, nb)

        Bx = drb("x", x_in)
        Bout = drb("out", out)
        BXT = drb("XT", XT)
        BQT = drb("QT", QT)
        BKT = drb("KT", KT)
        BVV = drb("VV", VV)
        BGT = drb("GT", GT)
        BMA = drb("MA", MA)
        BMB = drb("MB", MB)
        BCS = drb("CS", CS)
        Bparam = drb("params", None)
        BWIN = {l: drb("WIN%d" % l, WINb[l], True) for l in layers}
        BWF = {l: drb("WF%d" % l, WFb[l], True) for l in layers}
        BWOUT = {l: drb("WOUT%d" % l, WOUTb[l], True) for l in layers}
        BWUP = {l: drb("WUP%d" % l, WUPb[l], True) for l in layers}
        BWDN = {l: drb("WDN%d" % l, WDNb[l], True) for l in layers}

        PS = []
        for i in range(8):
            t = es.enter_context(nc.psum_tensor("ps%d" % i, [128, 512], F32))
            PS.append(Buf("ps%d" % i, t, True))

        ones_bf = sbt(es, "ones_bf", [128, 128], BF16, True)
        ones_f = sbt(es, "ones_f", [128, 512], F32, True)
        ident_f = sbt(es, "ident_f", [128, 128], F32, True)
        ident_bf = sbt(es, "ident_bf", [128, 128], BF16, True)
        gv = sbt(es, "gv", [128, 4 * 64], F32, True)
        cv = sbt(es, "cv", [128, 3 * 32], F32, True)
        cw = sbt(es, "cw", [128, 1024], F32, True)
        fw = sbt(es, "fw", [128, 1152], F32, True)
        negb = sbt(es, "negb", [128, 4], F32, True)
        lfh = {}
        halo = sbt(es, "halo", [128, 2 * FC, 2], F32, True)

        def mm(ps_ap, lhsT, rhs, start, stop, reads, writes):
            P.op(PE, lambda e: e.matmul(ps_ap, lhsT=lhsT, rhs=rhs, start=start, stop=stop), reads, writes)

        def act(out_ap, in_ap, func, reads, writes, scale=1.0, bias=0.0):
            P.op(ACT, lambda e: e.activation(out=out_ap, in_=in_ap, func=func, bias=bias, scale=scale), reads, writes)

        def tcopy(eng, out_ap, in_ap, reads, writes):
            P.op(eng, lambda e: e.tensor_copy(out=out_ap, in_=in_ap), reads, writes)

        def tt(eng, out_ap, a, b, op, reads, writes):
            P.op(eng, lambda e: e.tensor_tensor(out=out_ap, in0=a, in1=b, op=op), reads, writes)

        def tsc(eng, out_ap, a, s1, s2, op0, op1, reads, writes):
            if op1 is None:
                P.op(eng, lambda e: e.tensor_scalar(out=out_ap, in0=a, scalar1=s1, scalar2=None, op0=op0), reads, writes)
            else:
                P.op(eng, lambda e: e.tensor_scalar(out=out_ap, in0=a, scalar1=s1, scalar2=s2, op0=op0, op1=op1), reads, writes)

        def stt(eng, out_ap, a, s, b, op0, op1, reads, writes):
            P.op(eng, lambda e: e.scalar_tensor_tensor(out=out_ap, in0=a, scalar=s, in1=b, op0=op0, op1=op1), reads, writes)

        def dma(q, out_ap, in_ap, src, dst, chain=True, **kw):
            P.dma(q, lambda e: e.dma_start(out=out_ap, in_=in_ap, **kw), src, dst, chain=chain)

        def prep_units(l):
            units = []
            for g in range(20):
                units.append((WINb[l][g], w_in[l, :, g * GW:(g + 1) * GW].rearrange("(c p) n -> p c n", p=128), BWIN[l]))
            units.append((WFb[l][:, :, :], w_in[l, :, 5120:5128].rearrange("(c p) n -> p c n", p=128), BWF[l]))
            for g in range(8):
                units.append((WOUTb[l][g], w_out[l, :, g * GW:(g + 1) * GW].rearrange("(c p) n -> p c n", p=128), BWOUT[l]))
            for g in range(FC):
                units.append((WUPb[l][g, :, :, 0:128], w_up[l, :, g * 128:(g + 1) * 128].rearrange("(c p) n -> p c n", p=128), BWUP[l]))
                units.append((WUPb[l][g, :, :, 128:256], w_up[l, :, DFF + g * 128:DFF + (g + 1) * 128].rearrange("(c p) n -> p c n", p=128), BWUP[l]))
            for g in range(DC):
                units.append((WDNb[l][g], w_down[l, :, g * 128:(g + 1) * 128].rearrange("(c p) n -> p c n", p=128), BWDN[l]))
            return units

        prep_q = []

        def prep_pull(n):
            for _ in range(n):
                if not prep_q:
                    return
                o, i, b = prep_q.pop(0)
                dma(POOL, o, i, Bparam, b, chain=True, max_dma_last_dim=2048)

        def phase_S():
            with ExitStack() as sc:
                ld = [sbt(sc, "ld%d" % i, [128, 128], F32) for i in range(2)]
                P.op(POOL, lambda e: e.memset(ones_f.t[:, :], 1.0), [], [ones_f])
                P.op(DVE, lambda e: e.memset(ones_bf.t[:, :], 1.0), [], [ones_bf])
                P.op(POOL, lambda e: e.memset(ident_f.t[:, :], 1.0), [], [ident_f])
                P.op(POOL, lambda e: e.affine_select(out=ident_f.t[:, :], in_=ident_f.t[:, :], pattern=[[-1, 128]],
                                                      compare_op=ALU.is_equal, fill=0.0, base=0, channel_multiplier=1),
                     [ident_f], [ident_f])
                tcopy(DVE, ident_bf.t[:, :], ident_f.t[:, :], [ident_f], [ident_bf])
                P.op(DVE, lambda e: e.memset(halo.t[:, :, :], 0.0), [], [halo])
                P.op(DVE, lambda e: e.memset(ld[0].t[:, :], 0.0), [], [ld[0]])
                P.op(DVE, lambda e: e.memset(ld[1].t[:, :], 0.0), [], [ld[1]])
                cnt = [0]

                def rows_T(src_ap, nrows, ncols, dst_buf, dst_ap, neg=False):
                    k = cnt[0] % 2
                    cnt[0] += 1
                    dma(SP, ld[k].t[0:nrows, 0:ncols], src_ap, Bparam, ld[k])
                    ps = PS[k]
                    P.op(PE, lambda e: e.transpose(ps.t[:, 0:128], ld[k].t[:, :], ident_f.t[:, :]), [ld[k], ident_f], [ps])
                    if neg:
                        tsc(DVE, dst_ap, ps.t[0:ncols, 0:nrows], -1.0, None, ALU.mult, None, [ps], [dst_buf])
                    else:
                        tcopy(DVE, dst_ap, ps.t[0:ncols, 0:nrows], [ps], [dst_buf])

                for kind, g in enumerate([pre_mix_g, post_mix_g, pre_ffn_g, post_ffn_g]):
                    rows_T(g.ap().rearrange("l (c p) -> (l c) p", p=128), 16 * L, 128, gv, gv.t[:, kind * 64:kind * 64 + 16 * L])
                for kind, g in enumerate([conv_b, conv_ln_g, conv_ln_b]):
                    rows_T(g.ap().rearrange("l (c p) -> (l c) p", p=128), 8 * L, 128, cv, cv.t[:, kind * 32:kind * 32 + 8 * L])
                cwv = conv_w.ap().rearrange("l k (c p) -> (l k c) p", p=128)
                for r0 in range(0, L * CK * 8, 128):
                    n = min(128, L * CK * 8 - r0)
                    rows_T(cwv[r0:r0 + n, :], n, 128, cw, cw.t[:, r0:r0 + n])
                fwv = ffn_conv_w.ap().rearrange("l k (c p) -> (l k c) p", p=128)
                for r0 in range(0, L * 3 * 88, 128):
                    n = min(128, L * 3 * 88 - r0)
                    rows_T(fwv[r0:r0 + n, :], n, 128, fw, fw.t[:, r0:r0 + n])
                rows_T(b_forget.ap(), L, 8, negb, negb.t[0:8, 0:L], neg=True)
                P.emit()

        def gcol(kind, l, c):
            return gv.t[:, kind * 64 + l * 16 + c:kind * 64 + l * 16 + c + 1]

        def phase_T():
            with ExitStack() as sc:
                xin = [sbt(sc, "xin%d" % i, [128, D], F32) for i in range(4)]
                st = [sbt(sc, "xst%d" % i, [128, DC, TT], F32) for i in range(2)]
                XTv = XT.ap().rearrange("(c p) t -> p c t", p=128)
                k = 0
                for ti in range(NT):
                    sb = st[ti % 2]
                    for s in range(4):
                        xb = xin[k % 4]
                        r0 = ti * TT + s * 128
                        dma(SP, xb.t[:, :], x_in[r0:r0 + 128, :], Bx, xb)
                        for c4 in range(4):
                            ps = PS[(k * 4 + c4) % 8]
                            for j in range(4):
                                c = c4 * 4 + j
                                P.op(PE, (lambda e, ps=ps, xb=xb, c=c, j=j: e.transpose(
                                    ps.t[:, j * 128:(j + 1) * 128], xb.t[:, c * 128:(c + 1) * 128], ident_f.t[:, :])),
                                     [xb, ident_f], [ps])
                            eng = ACT if c4 % 2 == 0 else DVE
                            o = sb.t[:, c4 * 4:(c4 + 1) * 4, s * 128:(s + 1) * 128]
                            i_ = ps.t[:, :].rearrange("p (j t) -> p j t", j=4)
                            if eng is ACT:
                                P.op(ACT, (lambda e, o=o, i_=i_: e.copy(out=o, in_=i_)), [ps], [sb])
                            else:
                                tcopy(DVE, o, i_, [ps], [sb])
                        k += 1
                    dma(ACT, XTv[:, :, ti * TT:(ti + 1) * TT], sb.t[:, :, :], sb, BXT)
                P.emit()

        def phase_U():
            with ExitStack() as sc:
                xt = [sbt(sc, "uxt%d" % i, [128, DC, TT], F32) for i in range(2)]
                os_ = [sbt(sc, "uos%d" % i, [128, D], F32) for i in range(2)]
                XTv = XT.ap().rearrange("(c p) t -> p c t", p=128)
                k = 0
                for ti in range(NT):
                    xb = xt[ti % 2]
                    dma(SP, xb.t[:, :, :], XTv[:, :, ti * TT:(ti + 1) * TT], BXT, xb)
                    for s in range(4):
                        ob = os_[k % 2]
                        for c4 in range(4):
                            ps = PS[(k * 4 + c4) % 8]
                            for j in range(4):
                                c = c4 * 4 + j
                                P.op(PE, (lambda e, ps=ps, xb=xb, c=c, j=j, s=s: e.transpose(
                                    ps.t[:, j * 128:(j + 1) * 128], xb.t[:, c, s * 128:(s + 1) * 128], ident_f.t[:, :])),
                                     [xb, ident_f], [ps])
                            o = ob.t[:, c4 * 512:(c4 + 1) * 512]
                            if c4 % 2 == 0:
                                P.op(ACT, (lambda e, o=o, ps=ps: e.copy(out=o, in_=ps.t[:, :])), [ps], [ob])
                            else:
                                tcopy(DVE, o, ps.t[:, :], [ps], [ob])
                        r0 = ti * TT + s * 128
                        dma(ACT, out[r0:r0 + 128, :], ob.t[:, :], ob, Bout)
                        k += 1
                P.emit()

        def rms_r(psb, sq, rtmp, r, src_buf, src_chunk, nchunks, n_feat):
            for c in range(nchunks):
                s = sq[c % len(sq)]
                act(s.t[:, :], src_chunk(c), AF.Square, [src_buf], [s])
                mm(psb.t[:, :], ones_bf.t[:, :], s.t[:, :], c == 0, c == nchunks - 1, [ones_bf, s], [psb])
            act(rtmp.t[:, :], psb.t[:, :], AF.Sqrt, [psb], [rtmp], scale=1.0 / n_feat, bias=EPS)
            P.op(DVE, lambda e: e.reciprocal(out=r.t[:, :], in_=rtmp.t[:, :]), [rtmp], [r])

        def phase_A(l):
            lf_all = lfh["lf"]
            with ExitStack() as sc:
                xTs = [sbt(sc, "a_xT%d" % i, [128, DC, TT], F32) for i in range(2)]
                hTs = [sbt(sc, "a_hT%d" % i, [128, DC, TT], BF16) for i in range(2)]
                wsl = [sbt(sc, "a_w%d" % i, [128, DC, GW], BF16) for i in range(3)]
                wf = sbt(sc, "a_wf", [128, DC, 8], BF16)
                sq = [sbt(sc, "a_sq%d" % i, [128, TT], BF16) for i in range(2)]
                rtmp = sbt(sc, "a_rtmp", [128, TT], F32)
                r = sbt(sc, "a_r", [128, TT], F32)
                a_st = sbt(sc, "a_ast", [128, 8, TT], BF16)
                g_st = sbt(sc, "a_gst", [128, 8, TT], BF16)
                q_st = sbt(sc, "a_qst", [128, 8, TT], BF16)
                k_st = sbt(sc, "a_kst", [128, 8, TT], BF16)
                v_st = sbt(sc, "a_vst", [128, 4, CCH], BF16)
                sig = [sbt(sc, "a_sig%d" % i, [128, TT], F32) for i in range(2)]
                e1 = sbt(sc, "a_e1", [8, TT], F32)
                XTv = XT.ap().rearrange("(c p) t -> p c t", p=128)
                dma(SP, wf.t[:, :, :], WFb[l][:, :, :], BWF[l], wf)
                seq = [(ti, g) for ti in range(NT) for g in range(20)]
                state = {"next": 0}

                def wload():
                    i = state["next"]
                    if i >= len(seq):
                        return
                    g = seq[i][1]
                    dma(SP, wsl[i % 3].t[:, :, :], WINb[l][g], BWIN[l], wsl[i % 3])
                    state["next"] += 1

                psr = PS[0]
                psf = PS[1]
                pm = PS[2:8]
                pmi = [0]

                def nextps():
                    p = pm[pmi[0] % 6]
                    pmi[0] += 1
                    return p

                widx = 0
                dma(SP, xTs[0].t[:, :, :], XTv[:, :, 0:TT], BXT, xTs[0])

                def norm(tj):
                    xT_ = xTs[tj % 2]
                    hT_ = hTs[tj % 2]
                    rms_r(psr, sq, rtmp, r, xT_, lambda c, xT_=xT_: xT_.t[:, c, :], DC, D)
                    for c in range(DC):
                        stt(DVE, hT_.t[:, c, :], xT_.t[:, c, :], gcol(0, l, c), r.t[:, :], ALU.mult, ALU.mult, [xT_, gv, r], [hT_])

                wload()
                wload()
                norm(0)
                for ti in range(NT):
                    t0 = ti * TT
                    hT = hTs[ti % 2]
                    if ti + 1 < NT:
                        dma(SP, xTs[(ti + 1) % 2].t[:, :, :], XTv[:, :, t0 + TT:t0 + 2 * TT], BXT, xTs[(ti + 1) % 2])
                    for g in range(20):
                        if g == 12 and ti + 1 < NT:
                            norm(ti + 1)
                        w = wsl[widx % 3]
                        wload()
                        widx += 1
                        prep_pull(1)
                        if g < 16:
                            for j in range(2):
                                ch = g * 2 + j
                                ps = nextps()
                                for c in range(DC):
                                    mm(ps.t[:, :], w.t[:, c, j * 128:(j + 1) * 128], hT.t[:, c, :], c == 0, c == DC - 1, [w, hT], [ps])
                                if ch < 8:
                                    tcopy(DVE, a_st.t[:, ch, :], ps.t[:, :], [ps], [a_st])
                                elif ch < 16:
                                    sg = sig[ch % 2]
                                    act(sg.t[:, :], ps.t[:, :], AF.Sigmoid, [ps], [sg])
                                    tt(DVE, g_st.t[:, ch - 8, :], a_st.t[:, ch - 8, :], sg.t[:, :], ALU.mult, [a_st, sg], [g_st])
                                elif ch < 24:
                                    act(q_st.t[:, ch - 16, :], ps.t[:, :], AF.Copy, [ps], [q_st], scale=float(HD) ** -0.5)
                                else:
                                    tcopy(DVE, k_st.t[:, ch - 24, :], ps.t[:, :], [ps], [k_st])
                        else:
                            for s in range(4):
                                ps = nextps()
                                for c in range(DC):
                                    mm(ps.t[:, 0:GW], hT.t[:, c, s * 128:(s + 1) * 128], w.t[:, c, :], c == 0, c == DC - 1, [w, hT], [ps])
                                o = v_st.t[:, s, (g - 16) * GW:(g - 15) * GW]
                                if s % 2 == 0:
                                    P.op(ACT, (lambda e, o=o, ps=ps: e.copy(out=o, in_=ps.t[:, 0:GW])), [ps], [v_st])
                                else:
                                    tcopy(DVE, o, ps.t[:, 0:GW], [ps], [v_st])
                        if g == 7:
                            dma(POOL, GT.ap().rearrange("(c p) t -> p c t", p=128)[:, :, t0:t0 + TT], g_st.t[:, :, :], g_st, BGT)
                        if g == 11:
                            dma(POOL, QT.ap().rearrange("(c p) t -> p c t", p=128)[:, :, t0:t0 + TT], q_st.t[:, :, :], q_st, BQT)
                        if g == 15:
                            dma(POOL, KT.ap().rearrange("(c p) t -> p c t", p=128)[:, :, t0:t0 + TT], k_st.t[:, :, :], k_st, BKT)
                    dma(POOL, VV.ap().rearrange("(s p) e -> p s e", p=128)[:, ti * 4:(ti + 1) * 4, :], v_st.t[:, :, :], v_st, BVV)
                    for c in range(DC):
                        mm(psf.t[0:8, :], wf.t[:, c, :], hT.t[:, c, :], c == 0, c == DC - 1, [wf, hT], [psf])
                    act(e1.t[:, :], psf.t[0:8, :], AF.Exp, [psf, negb], [e1], scale=-1.0, bias=negb.t[0:8, l:l + 1])
                    act(lf_all.t[:, t0:t0 + TT], e1.t[:, :], AF.Ln, [e1], [lf_all], scale=1.0, bias=1.0)
                P.emit()

        def phase_B(l):
            lf_all = lfh["lf"]
            with ExitStack() as sc:
                cs = sbt(sc, "b_cs", [8, S], F32)
                r1 = sbt(sc, "b_r1", [8, S], F32)
                pcs = sbt(sc, "b_pcs", [8, 6, S], BF16)
                for i in range(NT):
                    init = 0.0 if i == 0 else cs.t[:, i * TT - 1:i * TT]
                    P.op(DVE, (lambda e, i=i, init=init: e.tensor_tensor_scan(
                        out=cs.t[:, i * TT:(i + 1) * TT], data0=ones_f.t[0:8, :], data1=lf_all.t[:, i * TT:(i + 1) * TT],
                        initial=init, op0=ALU.mult, op1=ALU.add)), [ones_f, lf_all, cs], [cs])
                tsc(DVE, pcs.t[:, 0, :], cs.t[:, :], -1.0, None, ALU.mult, None, [cs], [pcs])
                stt(DVE, r1.t[:, :], cs.t[:, :], -1.0, pcs.t[:, 0, :], ALU.mult, ALU.subtract, [cs, pcs], [r1])
                tcopy(DVE, pcs.t[:, 1, :], r1.t[:, :], [r1], [pcs])
                tt(DVE, cs.t[:, :], r1.t[:, :], pcs.t[:, 1, :], ALU.subtract, [r1, pcs], [cs])
                tcopy(DVE, pcs.t[:, 2, :], cs.t[:, :], [cs], [pcs])
                for j in range(3):
                    tsc(DVE, pcs.t[:, 3 + j, :], pcs.t[:, j, :], -1.0, None, ALU.mult, None, [pcs], [pcs])
                dma(SP, CS.ap(), pcs.t[:, :, :], pcs, BCS)
                P.emit()

        def phase_C(l):
            with ExitStack() as sc:
                qT = [sbt(sc, "c_q%d" % i, [128, S], BF16) for i in range(2)]
                kT = [sbt(sc, "c_k%d" % i, [128, S], BF16) for i in range(2)]
                vv = [sbt(sc, "c_v%d" % i, [128, 32, HD], BF16) for i in range(2)]
                bl = [sbt(sc, "c_bl%d" % i, [128, S], BF16) for i in range(2)]
                br = [sbt(sc, "c_br%d" % i, [128, S], BF16) for i in range(2)]
                pT = [sbt(sc, "c_p%d" % i, [128, TT], BF16) for i in range(3)]
                ost = [sbt(sc, "c_o%d" % i, [128, TT], BF16) for i in range(3)]
                rl = [sbt(sc, "c_rl%d" % i, [128, TT], F32) for i in range(2)]
                gl = [sbt(sc, "c_gl%d" % i, [128, 8, TT + 32], BF16) for i in range(2)]
                acc = sbt(sc, "c_acc", [128, 8, TT], F32)
                csq = sbt(sc, "c_sq", [128, TT], F32)
                mean = sbt(sc, "c_mean", [128, TT], F32)
                msq = sbt(sc, "c_msq", [128, TT], F32)
                var = sbt(sc, "c_var", [128, TT], F32)
                rstd = sbt(sc, "c_rstd", [128, TT], F32)
                tmp = [sbt(sc, "c_tmp%d" % i, [128, TT], F32) for i in range(2)]
                a_st = sbt(sc, "c_ast", [128, 8, TT], BF16)
                maskb = sbt(sc, "c_maskb", [128, 4, 512], BF16)
                mk = sbt(sc, "c_mk", [128, 4, 512], F32)
                P.op(POOL, lambda e: e.memset(mk.t[:, :, :], 0.0), [], [mk])
                for i in range(4):
                    P.op(POOL, (lambda e, i=i: e.affine_select(out=mk.t[:, i, :], in_=mk.t[:, i, :], pattern=[[1, 512]],
                                                                compare_op=ALU.is_ge, fill=NEG, base=-128 * i,
                                                                channel_multiplier=-1)), [mk], [mk])
                tcopy(DVE, maskb.t[:, :, :], mk.t[:, :, :], [mk], [maskb])
                for i in range(2):
                    P.op(DVE, (lambda e, i=i: e.memset(bl[i].t[:, :], 0.0)), [], [bl[i]])
                    P.op(DVE, (lambda e, i=i: e.memset(br[i].t[:, :], 0.0)), [], [br[i]])
                    P.op(DVE, (lambda e, i=i: e.memset(bl[i].t[0:6, :], 1.0)), [], [bl[i]])
                    P.op(DVE, (lambda e, i=i: e.memset(br[i].t[0:6, :], 1.0)), [], [br[i]])
                    P.op(DVE, (lambda e, i=i: e.memset(gl[i].t[:, :, :], 0.0)), [], [gl[i]])
                pst = PS[6]
                pcv = PS[6]
                dg = [sbt(sc, "c_dg%d" % i, [128, CK, 128], BF16) for i in range(2)]
                GTv = GT.ap().rearrange("(c p) t -> p c t", p=128)
                MAv = MA.ap().rearrange("(c p) t -> p c t", p=128)

                def conv_gen():
                    HAL = 32
                    cwl = cw.t[:, l * CK * 8:(l + 1) * CK * 8].rearrange("p (k c) -> p k c", c=8)

                    def build(n):
                        cc_ = n % 8
                        dgt_ = dg[n % 2]
                        tt(DVE, dgt_.t[:, :, :], ident_bf.t[:, :].unsqueeze(1).broadcast_to([128, CK, 128]),
                           cwl[:, :, cc_].unsqueeze(2).broadcast_to([128, CK, 128]), ALU.mult, [ident_bf, cw], [dgt_])

                    build(0)
                    for ti in range(NT):
                        g = gl[ti % 2]
                        t0 = ti * TT
                        if ti == 0:
                            dma(SP, g.t[:, :, HAL:HAL + TT], GTv[:, :, 0:TT], BGT, g)
                        else:
                            dma(SP, g.t[:, :, 0:HAL + TT], GTv[:, :, t0 - HAL:t0 + TT], BGT, g)
                        yield
                        for cc in range(8):
                            n_ = ti * 8 + cc
                            dgt = dg[n_ % 2]
                            if n_ + 1 < NT * 8:
                                build(n_ + 1)
                            for k in range(CK):
                                mm(pcv.t[:, :], dgt.t[:, k, :], g.t[:, cc, 2 + k:2 + k + TT], k == 0, k == CK - 1, [dgt, g], [pcv])
                            act(acc.t[:, cc, :], pcv.t[:, :], AF.Identity, [pcv, cv], [acc], scale=1.0,
                                bias=cv.t[:, l * 8 + cc:l * 8 + cc + 1])
                            yield
                        for cc in range(8):
                            mm(pst.t[:, :], ones_f.t[:, 0:128], acc.t[:, cc, :], cc == 0, cc == 7, [ones_f, acc], [pst])
                        act(mean.t[:, :], pst.t[:, :], AF.Copy, [pst], [mean], scale=1.0 / CCH)
                        act(msq.t[:, :], pst.t[:, :], AF.Square, [pst], [msq], scale=1.0 / CCH)
                        for cc in range(8):
                            act(csq.t[:, :], acc.t[:, cc, :], AF.Square, [acc], [csq])
                            mm(pst.t[:, :], ones_f.t[:, 0:128], csq.t[:, :], cc == 0, cc == 7, [ones_f, csq], [pst])
                        yield
                        stt(DVE, var.t[:, :], pst.t[:, :], 1.0 / CCH, msq.t[:, :], ALU.mult, ALU.subtract, [pst, msq], [var])
                        act(var.t[:, :], var.t[:, :], AF.Sqrt, [var], [var], scale=1.0, bias=EPS)
                        P.op(DVE, lambda e: e.reciprocal(out=rstd.t[:, :], in_=var.t[:, :]), [var], [rstd])
                        for cc in range(8):
                            tm = tmp[cc % 2]
                            tt(DVE, tm.t[:, :], acc.t[:, cc, :], mean.t[:, :], ALU.subtract, [acc, mean], [tm])
                            tt(DVE, tm.t[:, :], tm.t[:, :], rstd.t[:, :], ALU.mult, [tm, rstd], [tm])
                            act(a_st.t[:, cc, :], tm.t[:, :], AF.Silu, [tm, cv], [a_st],
                                scale=cv.t[:, 32 + l * 8 + cc:32 + l * 8 + cc + 1], bias=cv.t[:, 64 + l * 8 + cc:64 + l * 8 + cc + 1])
                        dma(POOL, MAv[:, :, t0:t0 + TT], a_st.t[:, :, :], a_st, BMA)
                        yield
                    while True:
                        yield

                cg = conv_gen()
                n_units = NT * (1 + 8 + 2)
                pulled = [0]

                def conv_pull(target):
                    while pulled[0] < target:
                        next(cg)
                        pulled[0] += 1

                def head_load(h):
                    sl = h % 2
                    dma(SP, qT[sl].t[:, :], QT[h * 128:(h + 1) * 128, :], BQT, qT[sl])
                    dma(SP, kT[sl].t[:, :], KT[h * 128:(h + 1) * 128, :], BKT, kT[sl])
                    dma(SP, vv[sl].t[:, :, :], VV.ap().rearrange("(b p) e -> p b e", p=128)[:, :, h * HD:(h + 1) * HD], BVV, vv[sl])
                    dma(SP, br[sl].t[0:3, :], CS[h, 0:3, :], BCS, br[sl])
                    dma(SP, bl[sl].t[3:6, :], CS[h, 3:6, :], BCS, bl[sl])

                head_load(0)
                qtc = 0
                total_blocks = NH * sum(4 * j + 4 for j in range(NT))
                done_blocks = 0
                for h in range(NH):
                    sl = h % 2
                    if h + 1 < NH:
                        head_load(h + 1)
                    for j in range(NT):
                        q0 = j * TT
                        nb = 4 * j + 4
                        po = PS[3 + (qtc % 2)]
                        pl = PS[5] if qtc % 2 == 0 else PS[7]

                        def QK(i):
                            ps = PS[i % 3]
                            diag = i >= 4 * j
                            c0 = 128 * (i - 4 * j) if diag else 0
                            mm(ps.t[:, c0:TT], kT[sl].t[:, i * 128:(i + 1) * 128], qT[sl].t[:, q0 + c0:q0 + TT], True, False, [kT[sl], qT[sl]], [ps])
                            mm(ps.t[:, c0:TT], bl[sl].t[:, i * 128:(i + 1) * 128], br[sl].t[:, q0 + c0:q0 + TT], False, not diag, [bl[sl], br[sl]], [ps])
                            if diag:
                                mm(ps.t[:, c0:TT], ident_bf.t[:, :], maskb.t[:, i - 4 * j, c0:TT], False, True, [ident_bf, maskb], [ps])

                        def PV(i):
                            p = pT[i % 3]
                            c0 = 128 * (i - 4 * j) if i >= 4 * j else 0
                            act(p.t[:, c0:TT], PS[i % 3].t[:, c0:TT], AF.Exp, [PS[i % 3]], [p])
                            mm(po.t[:, c0:TT], vv[sl].t[:, i, :], p.t[:, c0:TT], i == 0, i == nb - 1, [vv[sl], p], [po])
                            mm(pl.t[:, c0:TT], ones_bf.t[:, :], p.t[:, c0:TT], i == 0, i == nb - 1, [ones_bf, p], [pl])

                        QK(0)
                        if nb > 1:
                            QK(1)
                        for i in range(nb):
                            if i + 2 < nb:
                                QK(i + 2)
                            PV(i)
                        rr = rl[qtc % 2]
                        oo = ost[qtc % 3]
                        P.op(DVE, (lambda e, rr=rr, pl=pl: e.reciprocal(out=rr.t[:, :], in_=pl.t[:, :])), [pl], [rr])
                        tt(DVE, oo.t[:, :], po.t[:, :], rr.t[:, :], ALU.mult, [po, rr], [oo])
                        dma(POOL, MB[h * 128:(h + 1) * 128, q0:q0 + TT], oo.t[:, :], oo, BMB)
                        qtc += 1
                        done_blocks += nb
                        conv_pull(int(n_units * done_blocks / total_blocks) + 1)
                conv_pull(n_units + 4)
                P.emit()

        def phase_D(l, last):
            with ExitStack() as sc:
                xT = sbt(sc, "d_xT", [128, DC, TT], F32)
                hT = sbt(sc, "d_hT", [128, DC, TT], BF16)
                aT = sbt(sc, "d_aT", [128, FC, TT], BF16)
                ys = sbt(sc, "d_ys", [128, DC, TT], F32)
                ysc = [Buf("ysc%d" % i, ys.t) for i in range(DC)]
                wsl = [sbt(sc, "d_w%d" % i, [128, DC, GW], BF16) for i in range(3)]
                wdn = [sbt(sc, "d_wd%d" % i, [128, FC // 2, 128], BF16) for i in range(4)]
                sq = [sbt(sc, "d_sq%d" % i, [128, TT], BF16) for i in range(2)]
                rtmp = sbt(sc, "d_rtmp", [128, TT], F32)
                r = sbt(sc, "d_r", [128, TT], F32)
                yb = [sbt(sc, "d_yb%d" % i, [128, TT + 2], F32) for i in range(2)]
                zc = [sbt(sc, "d_zc%d" % i, [128, TT], F32) for i in range(4)]
                P.op(DVE, lambda e: e.memset(halo.t[:, :, :], 0.0), [], [halo])
                XTv = XT.ap().rearrange("(c p) t -> p c t", p=128)
                MAv = MA.ap().rearrange("(c p) t -> p c t", p=128)
                MBv = MB.ap().rearrange("(c p) t -> p c t", p=128)
                seq = []
                for ti in range(NT):
                    seq += [("o", g) for g in range(8)] + [("u", g) for g in range(FC)]
                state = {"next": 0, "dnext": 0}

                def wload():
                    i = state["next"]
                    if i >= len(seq):
                        return
                    kind, g = seq[i]
                    if kind == "o":
                        dma(SP, wsl[i % 3].t[:, :, :], WOUTb[l][g], BWOUT[l], wsl[i % 3])
                    else:
                        dma(SP, wsl[i % 3].t[:, :, :], WUPb[l][g], BWUP[l], wsl[i % 3])
                    state["next"] += 1

                def dload():
                    i = state["dnext"]
                    if i >= NT * DC * 2:
                        return
                    dc_, hf = (i // 2) % DC, i % 2
                    dma(SP, wdn[i % 4].t[:, :, :], WDNb[l][dc_][:, hf * 22:(hf + 1) * 22, :], BWDN[l], wdn[i % 4])
                    state["dnext"] += 1

                gcnt = [0]

                def gtick():
                    gcnt[0] += 1
                    if gcnt[0] % 4 == 0:
                        prep_pull(1)

                psr = PS[0]
                pm = PS[1:8]
                pmi = [0]

                def nextps():
                    p = pm[pmi[0] % 7]
                    pmi[0] += 1
                    return p

                widx = 0
                didx = 0
                wload()
                wload()
                dload()
                dload()
                dload()
                for ti in range(NT):
                    t0 = ti * TT
                    dma(SP, hT.t[:, 0:8, :], MAv[:, :, t0:t0 + TT], BMA, hT)
                    dma(SP, hT.t[:, 8:16, :], MBv[:, :, t0:t0 + TT], BMB, hT)
                    chunk = 0
                    pend = None
                    for g in range(8):
                        w = wsl[widx % 3]
                        wload()
                        widx += 1
                        gtick()
                        for j in range(2):
                            ps = nextps()
                            for c in range(DC):
                                mm(ps.t[:, :], w.t[:, c, j * 128:(j + 1) * 128], hT.t[:, c, :], c == 0, c == DC - 1, [w, hT], [ps])
                            if pend is not None:
                                mm(psr.t[:, :], ones_bf.t[:, :], pend[0].t[:, :], pend[1] == 0, pend[1] == DC - 1, [ones_bf, pend[0]], [psr])
                            if chunk % 2 == 0:
                                P.op(ACT, (lambda e, ps=ps, chunk=chunk: e.copy(out=ys.t[:, chunk, :], in_=ps.t[:, :])), [ps], [ysc[chunk]])
                            else:
                                tcopy(DVE, ys.t[:, chunk, :], ps.t[:, :], [ps], [ysc[chunk]])
                            s = sq[chunk % 2]
                            act(s.t[:, :], ys.t[:, chunk, :], AF.Square, [ysc[chunk]], [s])
                            pend = (s, chunk)
                            chunk += 1
                    mm(psr.t[:, :], ones_bf.t[:, :], pend[0].t[:, :], pend[1] == 0, pend[1] == DC - 1, [ones_bf, pend[0]], [psr])
                    dma(SP, xT.t[:, :, :], XTv[:, :, t0:t0 + TT], BXT, xT)
                    act(rtmp.t[:, :], psr.t[:, :], AF.Sqrt, [psr], [rtmp], scale=1.0 / D, bias=EPS)
                    P.op(DVE, lambda e: e.reciprocal(out=r.t[:, :], in_=rtmp.t[:, :]), [rtmp], [r])
                    for c in range(DC):
                        stt(DVE, ys.t[:, c, :], ys.t[:, c, :], gcol(1, l, c), r.t[:, :], ALU.mult, ALU.mult, [ysc[c], gv, r], [ysc[c]])
                        tt(DVE, xT.t[:, c, :], xT.t[:, c, :], ys.t[:, c, :], ALU.add, [xT, ysc[c]], [xT])
                        s = sq[c % 2]
                        act(s.t[:, :], xT.t[:, c, :], AF.Square, [xT], [s])
                        mm(psr.t[:, :], ones_bf.t[:, :], s.t[:, :], c == 0, c == DC - 1, [ones_bf, s], [psr])
                    act(rtmp.t[:, :], psr.t[:, :], AF.Sqrt, [psr], [rtmp], scale=1.0 / D, bias=EPS)
                    P.op(DVE, lambda e: e.reciprocal(out=r.t[:, :], in_=rtmp.t[:, :]), [rtmp], [r])
                    for c in range(DC):
                        stt(DVE, hT.t[:, c, :], xT.t[:, c, :], gcol(2, l, c), r.t[:, :], ALU.mult, ALU.mult, [xT, gv, r], [hT])
                    for g in range(FC):
                        w = wsl[widx % 3]
                        wload()
                        widx += 1
                        gtick()
                        zz = []
                        for j in range(2):
                            fci = g + j * FC
                            ps = nextps()
                            for c in range(DC):
                                mm(ps.t[:, :], w.t[:, c, j * 128:(j + 1) * 128], hT.t[:, c, :], c == 0, c == DC - 1, [w, hT], [ps])
                            y = yb[(g * 2 + j) % 2]
                            z = zc[(g * 2 + j) % 4]
                            tcopy(POOL, y.t[:, 0:2], halo.t[:, fci, :], [halo], [y])
                            P.op(ACT, (lambda e, y=y, ps=ps: e.copy(out=y.t[:, 2:TT + 2], in_=ps.t[:, :])), [ps], [y])
                            tcopy(POOL, halo.t[:, fci, :], y.t[:, TT:TT + 2], [y], [halo])
                            fcol = lambda k, fci=fci: fw.t[:, (l * 3 + k) * 88 + fci:(l * 3 + k) * 88 + fci + 1]
                            tsc(DVE, z.t[:, :], y.t[:, 0:TT], fcol(0), None, ALU.mult, None, [y, fw], [z])
                            stt(DVE, z.t[:, :], y.t[:, 1:TT + 1], fcol(1), z.t[:, :], ALU.mult, ALU.add, [y, fw, z], [z])
                            stt(DVE, z.t[:, :], y.t[:, 2:TT + 2], fcol(2), z.t[:, :], ALU.mult, ALU.add, [y, fw, z], [z])
                            zz.append(z)
                        act(zz[0].t[:, :], zz[0].t[:, :], AF.Gelu_apprx_tanh, [zz[0]], [zz[0]])
                        tt(DVE, aT.t[:, g, :], zz[0].t[:, :], zz[1].t[:, :], ALU.mult, [zz[0], zz[1]], [aT])
                    pend = None
                    for dc in range(DC):
                        gtick()
                        ps = nextps()
                        for hf in range(2):
                            w = wdn[didx % 4]
                            dload()
                            didx += 1
                            for f in range(FC // 2):
                                ff = hf * (FC // 2) + f
                                mm(ps.t[:, :], w.t[:, f, :], aT.t[:, ff, :], ff == 0, ff == FC - 1, [w, aT], [ps])
                        if pend is not None:
                            mm(psr.t[:, :], ones_bf.t[:, :], pend[0].t[:, :], pend[1] == 0, pend[1] == DC - 1, [ones_bf, pend[0]], [psr])
                        if dc % 2 == 0:
                            P.op(ACT, (lambda e, ps=ps, dc=dc: e.copy(out=ys.t[:, dc, :], in_=ps.t[:, :])), [ps], [ysc[dc]])
                        else:
                            tcopy(DVE, ys.t[:, dc, :], ps.t[:, :], [ps], [ysc[dc]])
                        s = sq[dc % 2]
                        act(s.t[:, :], ys.t[:, dc, :], AF.Square, [ysc[dc]], [s])
                        pend = (s, dc)
                    mm(psr.t[:, :], ones_bf.t[:, :], pend[0].t[:, :], pend[1] == 0, pend[1] == DC - 1, [ones_bf, pend[0]], [psr])
                    act(rtmp.t[:, :], psr.t[:, :], AF.Sqrt, [psr], [rtmp], scale=1.0 / D, bias=EPS)
                    P.op(DVE, lambda e: e.reciprocal(out=r.t[:, :], in_=rtmp.t[:, :]), [rtmp], [r])
                    for c in range(DC):
                        stt(DVE, ys.t[:, c, :], ys.t[:, c, :], gcol(3, l, c), r.t[:, :], ALU.mult, ALU.mult, [ysc[c], gv, r], [ysc[c]])
                        tt(DVE, xT.t[:, c, :], xT.t[:, c, :], ys.t[:, c, :], ALU.add, [xT, ysc[c]], [xT])
                    dma(POOL, XTv[:, :, t0:t0 + TT], xT.t[:, :, :], xT, BXT)
                P.emit()

        prep_q.extend(prep_units(layers[0]))
        phase_S()
        prep_pull(21)
        phase_T()
        for li, l in enumerate(layers):
            if li > 0:
                prep_pull(10 ** 6)
            with ExitStack() as sab:
                lfh["lf"] = sbt(sab, "lf_all", [8, S], F32, True)
                if "A" in phases:
                    phase_A(l)
                if "B" in phases:
                    phase_B(l)
            prep_pull(10 ** 6)
            if "C" in phases:
                phase_C(l)
            if li + 1 < len(layers):
                prep_q.extend(prep_units(layers[li + 1]))
            if "D" in phases:
                phase_D(l, li == len(layers) - 1)
        phase_U()
    return nc


_IN_NAMES = ["x", "pre_mix_g", "w_in", "b_forget", "conv_w", "conv_b", "conv_ln_g", "conv_ln_b", "w_out",
             "post_mix_g", "pre_ffn_g", "w_up", "ffn_conv_w", "w_down", "post_ffn_g"]


def run_layers(layers, inputs, x, dbg=False, phases="ABCD", ncores=8):
    layers = list(layers)
    nc = build_nc(list(range(len(layers))), dbg=dbg, phases=phases)
    shared = {k: np.ascontiguousarray(np.asarray(inputs[k], dtype=np.float32)[layers]) for k in _IN_NAMES if k != "x"}
    in_maps = []
    for b in range(ncores):
        m = dict(shared)
        m["x"] = np.ascontiguousarray(x[b])
        in_maps.append(m)
    res = run_bass_kernel_spmd(nc, in_maps, core_ids=list(range(ncores)))
    return res


def kernel(**inputs):
    x = np.asarray(inputs["x"], dtype=np.float32)
    res = run_layers(list(range(DEPTH)), inputs, x)
    return np.stack([np.asarray(r["out"], dtype=np.float32) for r in res.results], axis=0)
```
